# Optimizing a Trainium2 kernel written in Bass

```python
import jax, jax.numpy as jnp
from jax import lax
import numpy as np

D_MODEL = 1024
BATCH = 8
SEQ = 4096
DEPTH = 4

N_HEADS = 16
HEAD_DIM = D_MODEL // N_HEADS
D_FF = 2816
CHUNK = 64
LEFT_CHUNKS = 8
BAND = (LEFT_CHUNKS + 1) * CHUNK
REL_CLIP = 256
N_REL = 2 * REL_CLIP + 1
Q_BLOCK = 128
N_A = DEPTH // 2
N_B = DEPTH - N_A
EPS = 1e-6
NEG_INF = -1e30
ATTN_SCALE = HEAD_DIM ** -0.5
FFN_RES_WEIGHT = 0.5

kernel_name = "yoco_chunked_relpos_fox_macaron"


def rms_norm(x, g):
    xf = x.astype(jnp.float32)
    y = xf * lax.rsqrt(jnp.mean(xf * xf, axis=-1, keepdims=True) + EPS)
    return (y * g.astype(jnp.float32)).astype(x.dtype)


def swiglu(x, w_gate, w_up, w_down):
    return (jax.nn.silu(x @ w_gate) * (x @ w_up)) @ w_down


def chunked_relpos_attention(hn, w_qkv, w_o, rel_bias):
    b, s, _ = hn.shape
    nc = s // CHUNK
    q, k, v = jnp.split(hn @ w_qkv, 3, axis=-1)
    q = q.reshape(b, nc, CHUNK, N_HEADS, HEAD_DIM)
    k = k.reshape(b, nc, CHUNK, N_HEADS, HEAD_DIM)
    v = v.reshape(b, nc, CHUNK, N_HEADS, HEAD_DIM)

    def gather_band(t):
        tp = jnp.pad(t, ((0, 0), (LEFT_CHUNKS, 0), (0, 0), (0, 0), (0, 0)))
        return jnp.concatenate([tp[:, j:j + nc] for j in range(LEFT_CHUNKS + 1)], axis=2)

    kb, vb = gather_band(k), gather_band(v)
    qi = jnp.arange(CHUNK)[:, None]
    kj = jnp.arange(BAND)[None, :]
    rel = LEFT_CHUNKS * CHUNK + qi - kj
    rel_idx = jnp.clip(rel, -REL_CLIP, REL_CLIP) + REL_CLIP
    bias = rel_bias[:, rel_idx].astype(jnp.float32)
    key_chunk = jnp.arange(nc)[:, None] - LEFT_CHUNKS + jnp.arange(BAND)[None, :] // CHUNK
    valid = key_chunk >= 0

    logits = jnp.einsum('bcqhd,bckhd->bhcqk', q, kb).astype(jnp.float32) * ATTN_SCALE
    logits = jnp.where(valid[None, None, :, None, :], logits + bias[:, None], NEG_INF)
    p = jax.nn.softmax(logits, axis=-1).astype(vb.dtype)
    o = jnp.einsum('bhcqk,bckhd->bcqhd', p, vb).reshape(b, s, D_MODEL)
    return o @ w_o


def shared_kv_forget(h, kv_norm, w_kvf, b_f):
    b, s, _ = h.shape
    kvf = rms_norm(h, kv_norm) @ w_kvf
    k = kvf[..., :D_MODEL].reshape(b, s, N_HEADS, HEAD_DIM)
    v = kvf[..., D_MODEL:2 * D_MODEL].reshape(b, s, N_HEADS, HEAD_DIM)
    log_f = jax.nn.log_sigmoid(kvf[..., 2 * D_MODEL:].astype(jnp.float32) + b_f.astype(jnp.float32))
    cum_log_f = jnp.cumsum(log_f, axis=1).transpose(0, 2, 1)
    return k, v, cum_log_f


def forgetting_attention(hn, w_q, w_o, k, v, cum_log_f):
    b, s, _ = hn.shape
    q = (hn @ w_q).reshape(b, s, N_HEADS, HEAD_DIM)
    outs = []
    for blk in range(s // Q_BLOCK):
        q0, q1 = blk * Q_BLOCK, (blk + 1) * Q_BLOCK
        logits = jnp.einsum('bqhd,bkhd->bhqk', q[:, q0:q1], k[:, :q1]).astype(jnp.float32) * ATTN_SCALE
        decay = cum_log_f[:, :, q0:q1, None] - cum_log_f[:, :, None, :q1]
        causal = (q0 + jnp.arange(Q_BLOCK))[:, None] >= jnp.arange(q1)[None, :]
        logits = jnp.where(causal, logits + decay, NEG_INF)
        p = jax.nn.softmax(logits, axis=-1).astype(v.dtype)
        outs.append(jnp.einsum('bhqk,bkhd->bqhd', p, v[:, :q1]))
    o = jnp.concatenate(outs, axis=1).reshape(b, s, D_MODEL)
    return o @ w_o


def setup_inputs(seed: int = 0) -> dict:
    key = jax.random.key(seed)
    ks = jax.random.split(key, 16)
    f32 = jnp.float32
    nrm = lambda k, shape, fan_in: jax.random.normal(k, shape, f32) * (fan_in ** -0.5)
    return {
        "x": jax.random.normal(ks[0], (BATCH, SEQ, D_MODEL), f32),
        "ffn_norm": 1.0 + 0.05 * jax.random.normal(ks[1], (DEPTH, 2, D_MODEL), f32),
        "ffn_w_gate": nrm(ks[2], (DEPTH, 2, D_MODEL, D_FF), D_MODEL),
        "ffn_w_up": nrm(ks[3], (DEPTH, 2, D_MODEL, D_FF), D_MODEL),
        "ffn_w_down": nrm(ks[4], (DEPTH, 2, D_FF, D_MODEL), D_FF),
        "mix_norm": 1.0 + 0.05 * jax.random.normal(ks[5], (DEPTH, D_MODEL), f32),
        "a_w_qkv": nrm(ks[6], (N_A, D_MODEL, 3 * D_MODEL), D_MODEL),
        "a_w_o": nrm(ks[7], (N_A, D_MODEL, D_MODEL), D_MODEL),
        "a_rel_bias": 0.5 * jax.random.normal(ks[8], (N_A, N_HEADS, N_REL), f32),
        "kv_norm": 1.0 + 0.05 * jax.random.normal(ks[9], (D_MODEL,), f32),
        "b_w_kvf": nrm(ks[10], (D_MODEL, 2 * D_MODEL + N_HEADS), D_MODEL),
        "b_f_bias": 2.0 + 0.5 * jax.random.normal(ks[11], (N_HEADS,), f32),
        "b_w_q": nrm(ks[12], (N_B, D_MODEL, D_MODEL), D_MODEL),
        "b_w_o": nrm(ks[13], (N_B, D_MODEL, D_MODEL), D_MODEL),
        "final_norm": 1.0 + 0.05 * jax.random.normal(ks[14], (D_MODEL,), f32),
    }


def reference(x, ffn_norm, ffn_w_gate, ffn_w_up, ffn_w_down, mix_norm, a_w_qkv, a_w_o,
              a_rel_bias, kv_norm, b_w_kvf, b_f_bias, b_w_q, b_w_o, final_norm):
    def half_ffn(h, layer, pos):
        hn = rms_norm(h, ffn_norm[layer, pos])
        return h + FFN_RES_WEIGHT * swiglu(hn, ffn_w_gate[layer, pos], ffn_w_up[layer, pos],
                                           ffn_w_down[layer, pos])

    h = x
    for layer in range(N_A):
        h = half_ffn(h, layer, 0)
        h = h + chunked_relpos_attention(rms_norm(h, mix_norm[layer]), a_w_qkv[layer],
                                         a_w_o[layer], a_rel_bias[layer])
        h = half_ffn(h, layer, 1)

    k_sh, v_sh, cum_log_f = shared_kv_forget(h, kv_norm, b_w_kvf, b_f_bias)

    for lb in range(N_B):
        layer = N_A + lb
        h = half_ffn(h, layer, 0)
        h = h + forgetting_attention(rms_norm(h, mix_norm[layer]), b_w_q[lb], b_w_o[lb],
                                     k_sh, v_sh, cum_log_f)
        h = half_ffn(h, layer, 1)

    return rms_norm(h, final_norm)
```

```python
import numpy as np
from contextlib import ExitStack
import concourse.bass as bass
import concourse.mybir as mybir
from concourse.bass_utils import run_bass_kernel_spmd

F32 = mybir.dt.float32
BF16 = mybir.dt.bfloat16
AF = mybir.ActivationFunctionType
ALU = mybir.AluOpType

D = 1024
DC = 8
FF = 2816
NH = 16
DEPTH = 4
N_A = 2
EPS = 1e-6
SEQ = 4096
SLICES = [(0, 8), (8, 15), (15, 22)]
NEG = -30000.0
SB_BASE = 16512
SB_TOP = 229344
LOOKAHEAD = 2


def _merge(d, sem, val):
    k = id(sem)
    if k not in d or d[k][1] < val:
        d[k] = (sem, val)


class Buf:
    def __init__(self, name):
        self.name = name
        self.w = {}
        self.r = {}


class Owner:
    def __init__(self, sem):
        self.dsem = sem
        self.dcnt = 0


class Eng:
    def __init__(self, name, e, sem, is_pe=False):
        self.name = name
        self.e = e
        self.sem = sem
        self.n = 0
        self.seen = {}
        self.is_pe = is_pe

    def wait(self, sem, val):
        if self.is_pe and sem is self.sem:
            return
        k = id(sem)
        if self.seen.get(k, 0) >= val:
            return
        self.e.wait_ge(sem, val)
        self.seen[k] = val


class Prog:
    def __init__(self, S):
        self.S = S
        self.NT = S // 512
        self.NB = S // 128
        self.nc = bass.Bass("TRN2", target_bir_lowering=False)
        self.stack = ExitStack()
        self.sb_off = SB_BASE
        self.nsem = 0
        nc = self.nc
        self.pe = Eng("pe", nc.tensor, self.sem("pe"), is_pe=True)
        self.act = Eng("act", nc.scalar, self.sem("act"))
        self.dve = Eng("dve", nc.vector, self.sem("dve"))
        self.pool = Eng("pool", nc.gpsimd, self.sem("pool"))
        self.sp = Eng("sp", nc.sync, self.sem("sp"))

    def sem(self, name):
        self.nsem += 1
        return self.stack.enter_context(self.nc.semaphore("s_" + name))

    def owner(self, name):
        return Owner(self.sem(name))

    def sb(self, name, shape, dt):
        n = 1
        for s in shape[1:]:
            n *= s
        nbytes = n * (4 if dt == F32 else 2)
        off = self.sb_off
        self.sb_off += (nbytes + 31) // 32 * 32
        assert self.sb_off <= SB_TOP, (name, self.sb_off)
        return self.nc.alloc_sbuf_tensor_at(name, shape, dt, offset=off)

    def sb_at(self, name, shape, dt, off):
        return self.nc.alloc_sbuf_tensor_at(name, shape, dt, offset=off)

    def _waits(self, eng, reads, writes):
        for b in reads:
            for sem, val in b.w.values():
                eng.wait(sem, val)
        for b in writes:
            for sem, val in b.w.values():
                eng.wait(sem, val)
            for sem, val in b.r.values():
                eng.wait(sem, val)

    def _mark(self, sem, val, reads, writes):
        for b in reads:
            _merge(b.r, sem, val)
        for b in writes:
            _merge(b.w, sem, val)

    def op(self, eng, fn, reads=(), writes=()):
        self._waits(eng, reads, writes)
        ins = fn()
        eng.n += 1
        ins.then_inc(eng.sem, 1)
        self._mark(eng.sem, eng.n, reads, writes)

    def mm(self, mms, reads=(), writes=()):
        eng = self.pe
        self._waits(eng, reads, writes)
        ins = None
        for kw in mms:
            ins = self.nc.tensor.matmul(kw["out"], lhsT=kw["lhsT"], rhs=kw["rhs"],
                                        start=kw["start"], stop=kw["stop"])
        eng.n += 1
        ins.then_inc(eng.sem, 1)
        self._mark(eng.sem, eng.n, reads, writes)

    def dma(self, q, out, in_, reads, writes, owner):
        self._waits(q, reads, writes)
        ins = q.e.dma_start(out=out, in_=in_)
        owner.dcnt += 16
        ins.then_inc(owner.dsem, 16)
        self._mark(owner.dsem, owner.dcnt, reads, writes)

    def alias_barrier(self, old, new):
        deps = {}
        for b in old:
            for sem, val in list(b.w.values()) + list(b.r.values()):
                _merge(deps, sem, val)
        for b in new:
            for sem, val in deps.values():
                _merge(b.w, sem, val)

    def setup(self):
        nc, S, NT, NB = self.nc, self.S, self.NT, self.NB
        dt = nc.dram_tensor
        I = "ExternalInput"
        self.x_in = dt("xT", [128, DC, S], F32, kind=I).ap()
        self.gains_in = dt("gains", [128, 14 * DC], F32, kind=I).ap()
        self.wg_in = dt("ffn_w_gate", [DEPTH, 2, D, FF], F32, kind=I).ap()
        self.wu_in = dt("ffn_w_up", [DEPTH, 2, D, FF], F32, kind=I).ap()
        self.wd_in = dt("ffn_w_down", [DEPTH, 2, FF, D], F32, kind=I).ap()
        self.wqkv_in = dt("a_w_qkv", [N_A, D, 3 * D], F32, kind=I).ap()
        self.awo_in = dt("a_w_o", [N_A, D, D], F32, kind=I).ap()
        self.bt_in = dt("bt_full", [N_A, 8, 128, 2, 640], F32, kind=I).ap()
        self.wkvf_in = dt("b_w_kvf", [D, 2 * D + NH], F32, kind=I).ap()
        self.bf_in = dt("bf_bc", [128, NH], F32, kind=I).ap()
        self.bwq_in = dt("b_w_q", [2, D, D], F32, kind=I).ap()
        self.bwo_in = dt("b_w_o", [2, D, D], F32, kind=I).ap()
        self.c_ident_in = dt("c_ident", [128, 128], F32, kind=I).ap()
        self.c_ones_in = dt("c_ones", [128, 128], F32, kind=I).ap()
        self.c_mask_in = dt("c_mask", [128, 128], F32, kind=I).ap()
        self.c_tri_in = dt("c_tri", [128, 128], F32, kind=I).ap()
        self.c_sel_in = dt("c_sel", [128, 128], F32, kind=I).ap()
        self.c_esel_in = dt("c_esel", [NH, NH * 128], F32, kind=I).ap()
        self.out = dt("outT", [128, DC, S], F32, kind="ExternalOutput").ap()
        self.hD = dt("hD", [128, DC, S], F32).ap()
        self.hnD = dt("hnD", [128, DC, S], BF16).ap()
        self.qD = dt("qD", [8, 128, S], BF16).ap()
        self.kD = dt("kD", [8, 128, S], BF16).ap()
        self.vD = dt("vD", [8, NB, 128, 192], BF16).ap()
        self.oD = dt("oD", [128, DC, S], BF16).ap()
        self.c1D = dt("c1D", [NH, S], BF16).ap()
        mk = lambda n: [Buf(f"{n}{t}") for t in range(NT)]
        self.xb, self.hDb, self.hnDb, self.qDb, self.kDb, self.vDb, self.oDb, self.c1Db, self.outb = (
            mk("x"), mk("hD"), mk("hnD"), mk("qD"), mk("kD"), mk("vD"), mk("oD"), mk("c1D"), mk("out"))
        self.inb = Buf("inputs")

        self.ps_t = nc.alloc_psum_tensor("ps", [128, 8, 512], F32)
        self.ps = [Buf(f"ps{i}") for i in range(8)]

        self.W_t = [self.sb(f"W{i}", [128, 24576], BF16) for i in range(2)]
        self.Wb = [Buf(f"W{i}") for i in range(2)]
        self.Wo_ = [self.owner(f"W{i}") for i in range(2)]
        arena = self.sb_off
        self.hT_t = [self.sb(f"hT{i}", [128, DC, 512], F32) for i in range(2)]
        self.hn_t = [self.sb(f"hn{i}", [128, DC, 512], BF16) for i in range(2)]
        arena_end = self.sb_off
        self.hTb = [[Buf(f"hT{i}_{c}") for c in range(DC)] for i in range(2)]
        self.hnb = [[Buf(f"hn{i}_{c}") for c in range(DC)] for i in range(2)]
        self.hTo = [self.owner(f"hT{i}") for i in range(2)]
        self.hno = [self.owner(f"hn{i}") for i in range(2)]
        kvbytes = S * 2 + NB * 192 * 2
        self.kT_t, self.V_t, self.kvb, self.kvo = [], [], [], []
        off = arena
        for i in range(2):
            self.kT_t.append(self.sb_at(f"kT{i}", [128, S], BF16, off))
            self.V_t.append(self.sb_at(f"V{i}", [128, NB, 192], BF16, off + S * 2))
            off += kvbytes
            self.kvb.append([Buf(f"kv{i}_{t}") for t in range(NT)])
            self.kvo.append([self.owner(f"kv{i}_{t}") for t in range(NT)])
        self.qT_t, self.BT_t, self.qTb, self.BTb, self.qTo, self.BTo = [], [], [], [], [], []
        for i in range(2):
            self.qT_t.append(self.sb_at(f"qT{i}", [128, 512], BF16, off)); off += 1024
            self.qTb.append(Buf(f"qT{i}")); self.qTo.append(self.owner(f"qT{i}"))
        for i in range(2):
            self.BT_t.append(self.sb_at(f"BT{i}", [128, 2, 640], BF16, off)); off += 2560
            self.BTb.append(Buf(f"BT{i}")); self.BTo.append(self.owner(f"BT{i}"))
        assert off <= arena_end, (off, arena_end)
        self.c1_t, self.c1b, self.c1o = [], [], []
        for i in range(2):
            self.c1_t.append(self.sb(f"c1t{i}", [NH, 512], BF16))
            self.c1b.append(Buf(f"c1t{i}")); self.c1o.append(self.owner(f"c1t{i}"))
        self.ffn_view = [b for sl in self.hTb + self.hnb for b in sl]
        self.att_view = [b for sl in self.kvb for b in sl] + self.qTb + self.BTb

        self.sq_t = [self.sb(f"sq{i}", [128, 512], BF16) for i in range(4)]
        self.sqb = [Buf(f"sq{i}") for i in range(4)]
        self.act_t = self.sb("act", [128, 8, 512], BF16)
        self.actb = [Buf(f"act{i}") for i in range(8)]
        self.rstd_t = self.sb("rstd", [128, 512], F32)
        self.rstdb = Buf("rstd")
        self.silu_t = [self.sb(f"silu{i}", [128, 512], F32) for i in range(2)]
        self.silub = [Buf(f"silu{i}") for i in range(2)]
        self.g_t = self.sb("gains", [128, 14 * DC], F32)
        self.ident_t = self.sb("ident", [128, 128], BF16)
        self.ones_t = self.sb("ones", [128, 128], BF16)
        self.mask_t = self.sb("mask", [128, 128], BF16)
        self.esel_t = self.sb("esel", [NH, NH * 128], BF16)
        self.identf_t = self.sb("identf", [128, 128], F32)
        self.tri_t = self.sb("tri", [128, 128], F32)
        self.sel_t = self.sb("sel", [128, 128], F32)
        self.bf_t = self.sb("bfbc", [128, NH], F32)
        self.constb = Buf("consts")
        self.consto = self.owner("consts")
        self.negc_t = self.sb("negc", [128, NB, NH], F32)
        self.negcb = Buf("negc")
        self.stg_t = [self.sb(f"stg{i}", [128, 512], BF16) for i in range(8)]
        self.stgb = [Buf(f"stg{i}") for i in range(8)]
        self.stgo = [self.owner(f"stg{i}") for i in range(8)]
        self.stg_i = 0
        self.vstg_t = [self.sb(f"vstg{i}", [128, 8, 192], BF16) for i in range(2)]
        self.vstgb = [Buf(f"vstg{i}") for i in range(2)]
        self.vstgo = [self.owner(f"vstg{i}") for i in range(2)]
        self.pT_t = [self.sb(f"pT{i}", [128, 512], BF16) for i in range(4)]
        self.pTb = [Buf(f"pT{i}") for i in range(4)]
        self.rec_t = [self.sb(f"rec{i}", [128, 512], F32) for i in range(2)]
        self.recb = [Buf(f"rec{i}") for i in range(2)]
        self.oT_t = [self.sb(f"oT{i}", [128, 512], BF16) for i in range(2)]
        self.oTb = [Buf(f"oT{i}") for i in range(2)]
        self.oTo = [self.owner(f"oT{i}") for i in range(2)]
        self.z_t = self.sb("ztmp", [128, NH], F32)
        self.zb = Buf("ztmp")
        self.l_t = self.sb("ltmp", [128, NH], F32)
        self.lb = Buf("ltmp")
        self.c1stg_t = self.sb("c1stg", [NH, 512], BF16)
        self.c1stgb = Buf("c1stg")
        self.c1stgo = self.owner("c1stg")
        self.bank_i = 0

        sp, pool = self.sp, self.pool
        cb, co = self.constb, self.consto
        self.dma(sp, self.g_t[:], self.gains_in, [self.inb], [cb], co)
        self.dma(sp, self.identf_t[:], self.c_ident_in, [self.inb], [cb], co)
        self.dma(sp, self.tri_t[:], self.c_tri_in, [self.inb], [cb], co)
        self.dma(sp, self.sel_t[:], self.c_sel_in, [self.inb], [cb], co)
        self.dma(sp, self.bf_t[:], self.bf_in, [self.inb], [cb], co)
        self.dma(pool, self.ident_t[:], self.c_ident_in, [self.inb], [cb], co)
        self.dma(pool, self.ones_t[:], self.c_ones_in, [self.inb], [cb], co)
        self.dma(pool, self.mask_t[:], self.c_mask_in, [self.inb], [cb], co)
        self.dma(pool, self.esel_t[:], self.c_esel_in, [self.inb], [cb], co)
        for i in range(2):
            self.op(self.dve, lambda: nc.vector.memset(self.vstg_t[i][:, :, 64:128], 1.0), [], [self.vstgb[i]])

    def load_wset(self, ws, slot):
        W = self.W_t[slot]
        for (o, ncol, nch, src) in ws:
            dst = W[:, o:o + nch * ncol].rearrange("p (c f) -> p c f", c=nch)
            self.dma(self.pool, dst, src, [self.inb], [self.Wb[slot]], self.Wo_[slot])

    def wset_ffn(self, L, pos, si):
        f0, f1 = SLICES[si]
        nf = f1 - f0
        g = self.wg_in[L, pos].rearrange("(c p) f -> p c f", p=128)[:, :, f0 * 128:f1 * 128]
        u = self.wu_in[L, pos].rearrange("(c p) f -> p c f", p=128)[:, :, f0 * 128:f1 * 128]
        d = self.wd_in[L, pos].rearrange("(f p) d -> p f d", p=128)[:, f0:f1, :]
        return [(0, nf * 128, 8, g), (8192, nf * 128, 8, u), (16384, 1024, nf, d)]

    def wset_mat(self, src2d, ncols):
        return [(0, ncols, 8, src2d.rearrange("(c p) f -> p c f", p=128))]

    def Wview(self, slot, o, nch, ncol):
        return self.W_t[slot][:, o:o + nch * ncol].rearrange("p (c f) -> p c f", c=nch)

    def tsl(self, t):
        return slice(t * 512, (t + 1) * 512)

    def load_h(self, src, srcb, t, slot):
        self.dma(self.sp, self.hT_t[slot][:], src[:, :, self.tsl(t)], [srcb[t]], self.hTb[slot], self.hTo[slot])

    def load_hn(self, src, srcb, t, slot):
        self.dma(self.sp, self.hn_t[slot][:], src[:, :, self.tsl(t)], [srcb[t]], self.hnb[slot], self.hno[slot])

    def store_h(self, dst, dstb, t, slot):
        self.dma(self.sp, dst[:, :, self.tsl(t)], self.hT_t[slot][:], self.hTb[slot], [dstb[t]], self.hTo[slot])

    def norm(self, slot, gidx, final=False):
        nc = self.nc
        hT, hn = self.hT_t[slot], self.hn_t[slot]
        for c in range(DC):
            k = c % 4
            self.op(self.act, lambda: nc.scalar.activation(out=self.sq_t[k][:], in_=hT[:, c, :], func=AF.Square),
                    [self.hTb[slot][c]], [self.sqb[k]])
            self.mm([dict(out=self.ps_t[:, 7, :], lhsT=self.ones_t[:], rhs=self.sq_t[k][:], start=(c == 0), stop=(c == DC - 1))],
                    [self.sqb[k], self.constb], [self.ps[7]])
        self.op(self.act, lambda: nc.scalar.activation(out=self.rstd_t[:], in_=self.ps_t[:, 7, :], func=AF.Ln,
                                                       scale=1.0 / D, bias=EPS), [self.ps[7]], [self.rstdb])
        self.op(self.act, lambda: nc.scalar.activation(out=self.rstd_t[:], in_=self.rstd_t[:], func=AF.Exp, scale=-0.5),
                [self.rstdb], [self.rstdb])
        for c in range(DC):
            gc = self.g_t[:, gidx * DC + c:gidx * DC + c + 1]
            if final:
                self.op(self.dve, lambda: nc.vector.scalar_tensor_tensor(out=hT[:, c, :], in0=hT[:, c, :], scalar=gc,
                                                                         in1=self.rstd_t[:], op0=ALU.mult, op1=ALU.mult),
                        [self.hTb[slot][c], self.rstdb, self.constb], [self.hTb[slot][c]])
            else:
                self.op(self.dve, lambda: nc.vector.scalar_tensor_tensor(out=hn[:, c, :], in0=hT[:, c, :], scalar=gc,
                                                                         in1=self.rstd_t[:], op0=ALU.mult, op1=ALU.mult),
                        [self.hTb[slot][c], self.rstdb, self.constb], [self.hnb[slot][c]])

    def next_bank(self, n=6):
        b = self.bank_i % n
        self.bank_i += 1
        return b

    def ffn_pass(self, wslot, si, gidx, src, srcb, first):
        nc, NT = self.nc, self.NT
        f0, f1 = SLICES[si]
        nf = f1 - f0
        Wg = self.Wview(wslot, 0, 8, nf * 128)
        Wu = self.Wview(wslot, 8192, 8, nf * 128)
        Wd = self.Wview(wslot, 16384, nf, 1024)
        Wb = self.Wb[wslot]

        def prep(t):
            s = t % 2
            self.load_h(src, srcb, t, s)
            if not first:
                self.load_hn(self.hnD, self.hnDb, t, s)

        def do_norm(t):
            s = t % 2
            self.norm(s, gidx)
            self.dma(self.sp, self.hnD[:, :, self.tsl(t)], self.hn_t[s][:], self.hnb[s], [self.hnDb[t]], self.hno[s])

        prep(0)
        if first:
            do_norm(0)
        for t in range(NT):
            s = t % 2
            hn = self.hn_t[s]
            hT = self.hT_t[s]
            if t + 1 < NT:
                prep(t + 1)
            for fi in range(nf):
                gb, ub = fi % 2, 2 + fi % 2
                self.mm([dict(out=self.ps_t[:, gb, :], lhsT=Wg[:, c, fi * 128:(fi + 1) * 128], rhs=hn[:, c, :],
                              start=(c == 0), stop=(c == DC - 1)) for c in range(DC)],
                        self.hnb[s] + [Wb], [self.ps[gb]])
                self.mm([dict(out=self.ps_t[:, ub, :], lhsT=Wu[:, c, fi * 128:(fi + 1) * 128], rhs=hn[:, c, :],
                              start=(c == 0), stop=(c == DC - 1)) for c in range(DC)],
                        self.hnb[s] + [Wb], [self.ps[ub]])
                k = fi % 2
                self.op(self.act, lambda: nc.scalar.activation(out=self.silu_t[k][:], in_=self.ps_t[:, gb, :], func=AF.Silu),
                        [self.ps[gb]], [self.silub[k]])
                self.op(self.dve, lambda: nc.vector.tensor_tensor(out=self.act_t[:, fi, :], in0=self.ps_t[:, ub, :],
                                                                  in1=self.silu_t[k][:], op=ALU.mult),
                        [self.ps[ub], self.silub[k]], [self.actb[fi]])
            for dc in range(DC):
                if dc == 4 and first and t + 1 < NT:
                    do_norm(t + 1)
                db = 4 + dc % 2
                self.mm([dict(out=self.ps_t[:, db, :], lhsT=Wd[:, fi, dc * 128:(dc + 1) * 128], rhs=self.act_t[:, fi, :],
                              start=(fi == 0), stop=(fi == nf - 1)) for fi in range(nf)],
                        self.actb[:nf] + [Wb], [self.ps[db]])
                self.op(self.dve, lambda: nc.vector.scalar_tensor_tensor(out=hT[:, dc, :], in0=self.ps_t[:, db, :], scalar=0.5,
                                                                         in1=hT[:, dc, :], op0=ALU.mult, op1=ALU.add),
                        [self.ps[db], self.hTb[s][dc]], [self.hTb[s][dc]])
            self.store_h(self.hD, self.hDb, t, s)

    def evac_to_dram(self, bank, scale, dst_ap, dstb):
        nc = self.nc
        k = self.stg_i % 8
        self.stg_i += 1
        self.op(self.act, lambda: nc.scalar.activation(out=self.stg_t[k][:], in_=self.ps_t[:, bank, :], func=AF.Copy, scale=scale),
                [self.ps[bank]], [self.stgb[k]])
        self.dma(self.sp, dst_ap, self.stg_t[k][:], [self.stgb[k]], [dstb], self.stgo[k])

    def proj_fm(self, Wv, col0, s, t, scale, dstD, dstb):
        hn = self.hn_t[s]
        for oc in range(8):
            bank = self.next_bank()
            self.mm([dict(out=self.ps_t[:, bank, :], lhsT=Wv[:, c, col0 + oc * 128:col0 + (oc + 1) * 128], rhs=hn[:, c, :],
                          start=(c == 0), stop=(c == DC - 1)) for c in range(DC)],
                    self.hnb[s] + [self.Wb[self.cur_wslot]], [self.ps[bank]])
            self.evac_to_dram(bank, scale, dstD[oc, :, self.tsl(t)], dstb[t])

    def proj_v(self, Wv, col0, s, t):
        nc = self.nc
        hn = self.hn_t[s]
        for sub in range(4):
            blk = 4 * t + sub
            vs = blk % 2
            for hf in range(2):
                bank = self.next_bank()
                self.mm([dict(out=self.ps_t[:, bank, :], lhsT=hn[:, c, sub * 128:(sub + 1) * 128],
                              rhs=Wv[:, c, col0 + hf * 512:col0 + (hf + 1) * 512], start=(c == 0), stop=(c == DC - 1))
                         for c in range(DC)],
                        self.hnb[s] + [self.Wb[self.cur_wslot]], [self.ps[bank]])
                pv = self.ps_t[:, bank, :].rearrange("p (j e d) -> p j e d", j=4, e=2)
                for e in range(2):
                    self.op(self.dve, lambda: nc.vector.tensor_copy(out=self.vstg_t[vs][:, 4 * hf:4 * hf + 4, 128 * e:128 * e + 64],
                                                                    in_=pv[:, :, e, :]),
                            [self.ps[bank]], [self.vstgb[vs]])
            self.dma(self.sp, self.vD[:, blk].rearrange("h p c -> p h c"), self.vstg_t[vs][:], [self.vstgb[vs]],
                     [self.vDb[t]], self.vstgo[vs])

    def proj_f(self, Wv, s, t):
        nc = self.nc
        hn = self.hn_t[s]
        for sub in range(4):
            blk = 4 * t + sub
            self.mm([dict(out=self.ps_t[:, 6, 0:NH], lhsT=hn[:, c, sub * 128:(sub + 1) * 128], rhs=Wv[:, c, 2048:2048 + NH],
                          start=(c == 0), stop=(c == DC - 1)) for c in range(DC)],
                    self.hnb[s] + [self.Wb[self.cur_wslot]], [self.ps[6]])
            self.op(self.dve, lambda: nc.vector.tensor_tensor(out=self.z_t[:], in0=self.ps_t[:, 6, 0:NH], in1=self.bf_t[:], op=ALU.add),
                    [self.ps[6], self.constb], [self.zb])
            self.op(self.act, lambda: nc.scalar.activation(out=self.z_t[:], in_=self.z_t[:], func=AF.Exp, scale=-1.0),
                    [self.zb], [self.zb])
            self.op(self.act, lambda: nc.scalar.activation(out=self.l_t[:], in_=self.z_t[:], func=AF.Ln, bias=1.0),
                    [self.zb], [self.lb])
            mms = [dict(out=self.ps_t[:, 6, 16:16 + NH], lhsT=self.tri_t[:], rhs=self.l_t[:], start=True, stop=(blk == 0))]
            if blk > 0:
                mms.append(dict(out=self.ps_t[:, 6, 16:16 + NH], lhsT=self.sel_t[:], rhs=self.negc_t[:, blk - 1, :], start=False, stop=True))
            self.mm(mms, [self.lb, self.negcb, self.constb], [self.ps[6]])
            self.op(self.dve, lambda: nc.vector.tensor_copy(out=self.negc_t[:, blk, :], in_=self.ps_t[:, 6, 16:16 + NH]),
                    [self.ps[6]], [self.negcb])
            self.mm([dict(out=self.ps_t[0:NH, 6, 32:160], lhsT=self.negc_t[:, blk, :], rhs=self.identf_t[:], start=True, stop=True)],
                    [self.negcb, self.constb], [self.ps[6]])
            self.op(self.dve, lambda: nc.vector.tensor_scalar(out=self.c1stg_t[:, sub * 128:(sub + 1) * 128], in0=self.ps_t[0:NH, 6, 32:160],
                                                              scalar1=-1.0, scalar2=None, op0=ALU.mult),
                    [self.ps[6]], [self.c1stgb])
        self.dma(self.sp, self.c1D[:, self.tsl(t)], self.c1stg_t[:], [self.c1stgb], [self.c1Db[t]], self.c1stgo)

    def proj_pass(self, wslot, kind, gidx, ncols):
        NT = self.NT
        self.cur_wslot = wslot
        Wv = self.Wview(wslot, 0, 8, ncols)
        self.load_h(self.hD, self.hDb, 0, 0)
        self.norm(0, gidx)
        for t in range(NT):
            s = t % 2
            if t + 1 < NT:
                self.load_h(self.hD, self.hDb, t + 1, (t + 1) % 2)
            if kind == "A":
                self.proj_fm(Wv, 0, s, t, 0.125, self.qD, self.qDb)
                if t + 1 < NT:
                    self.norm((t + 1) % 2, gidx)
                self.proj_fm(Wv, 1024, s, t, 1.0, self.kD, self.kDb)
                self.proj_v(Wv, 2048, s, t)
            elif kind == "B":
                self.proj_fm(Wv, 0, s, t, 0.125, self.qD, self.qDb)
                if t + 1 < NT:
                    self.norm((t + 1) % 2, gidx)
            else:
                self.proj_fm(Wv, 0, s, t, 1.0, self.kD, self.kDb)
                if t + 1 < NT:
                    self.norm((t + 1) % 2, gidx)
                self.proj_v(Wv, 1024, s, t)
                self.proj_f(Wv, s, t)

    def out_proj_pass(self, wslot):
        nc, NT = self.nc, self.NT
        Wv = self.Wview(wslot, 0, 8, 1024)
        Wb = self.Wb[wslot]
        self.load_h(self.hD, self.hDb, 0, 0)
        self.load_hn(self.oD, self.oDb, 0, 0)
        for t in range(NT):
            s = t % 2
            if t + 1 < NT:
                self.load_h(self.hD, self.hDb, t + 1, (t + 1) % 2)
                self.load_hn(self.oD, self.oDb, t + 1, (t + 1) % 2)
            hT, o = self.hT_t[s], self.hn_t[s]
            for dc in range(DC):
                bank = self.next_bank()
                self.mm([dict(out=self.ps_t[:, bank, :], lhsT=Wv[:, c, dc * 128:(dc + 1) * 128], rhs=o[:, c, :],
                              start=(c == 0), stop=(c == DC - 1)) for c in range(DC)],
                        self.hnb[s] + [Wb], [self.ps[bank]])
                self.op(self.dve, lambda: nc.vector.tensor_tensor(out=hT[:, dc, :], in0=self.ps_t[:, bank, :], in1=hT[:, dc, :], op=ALU.add),
                        [self.ps[bank], self.hTb[s][dc]], [self.hTb[s][dc]])
            self.store_h(self.hD, self.hDb, t, s)

    def final_pass(self, do_norm=True):
        NT = self.NT
        self.load_h(self.hD, self.hDb, 0, 0)
        for t in range(NT):
            s = t % 2
            if t + 1 < NT:
                self.load_h(self.hD, self.hDb, t + 1, (t + 1) % 2)
            if do_norm:
                self.norm(s, 13, final=True)
            self.store_h(self.out, self.outb, t, s)

    def attn_pass(self, kind, la):
        nc, NT = self.nc, self.NT
        self.alias_barrier(self.ffn_view, self.att_view)
        pending = []
        state = dict(unit=0, pv=0)

        def load_kv(hp):
            sl = hp % 2
            for kt in range(NT):
                b, o = self.kvb[sl][kt], self.kvo[sl][kt]
                self.dma(self.sp, self.kT_t[sl][:, self.tsl(kt)], self.kD[hp, :, self.tsl(kt)], [self.kDb[kt]], [b], o)
                self.dma(self.sp, self.V_t[sl][:, 4 * kt:4 * kt + 4, :], self.vD[hp, 4 * kt:4 * kt + 4].rearrange("b p c -> p b c"),
                         [self.vDb[kt]], [b], o)
            if kind == "A":
                self.dma(self.pool, self.BT_t[sl][:], self.bt_in[la, hp], [self.inb], [self.BTb[sl]], self.BTo[sl])
                self.op(self.pool, lambda: nc.gpsimd.memset(self.BT_t[sl][0:64, :, 576:640], NEG), [], [self.BTb[sl]])
                self.op(self.pool, lambda: nc.gpsimd.memset(self.BT_t[sl][64:128, :, 0:64], NEG), [], [self.BTb[sl]])

        seq = [(hp, tt) for hp in range(8) for tt in range(NT)]

        def load_q(i):
            hp, tt = seq[i]
            qs = i % 2
            self.dma(self.sp, self.qT_t[qs][:], self.qD[hp, :, self.tsl(tt)], [self.qDb[tt]], [self.qTb[qs]], self.qTo[qs])
            if kind == "B":
                self.dma(self.sp, self.c1_t[qs][:], self.c1D[:, self.tsl(tt)], [self.c1Db[tt]], [self.c1b[qs]], self.c1o[qs])

        def flush(keep):
            while len(pending) > keep:
                pending.pop(0)()

        load_kv(0)
        load_q(0)
        for i, (hp, tt) in enumerate(seq):
            sl, qs = hp % 2, i % 2
            if i + 1 < len(seq):
                load_q(i + 1)
            if tt == 1 and hp + 1 < 8:
                load_kv(hp + 1)
            os_ = i % 2
            for e in range(2):
                h = 2 * hp + e
                pb = 64 * e
                if kind == "A":
                    units = []
                    order = [3, 0, 1, 2, 4, 5, 6, 7] if tt > 0 else [4, 5, 6, 7]
                    for j in order:
                        kb = 4 * tt - 4 + j
                        if j <= 3:
                            units.append((kb, 0, 128 * (j + 1)))
                        else:
                            units.append((kb, 128 * (j - 4), 512))
                else:
                    units = [(kb, 0, 512) for kb in range(4 * tt)] + [(4 * tt + j, 128 * j, 512) for j in range(4)]
                pvb = 4 + state["pv"] % 2
                state["pv"] += 1
                nu = len(units)
                for ui, (kb, qlo, qhi) in enumerate(units):
                    n = qhi - qlo
                    u = state["unit"]
                    state["unit"] += 1
                    sbk = u % 4
                    kt = kb // 4
                    mms = [dict(out=self.ps_t[:, sbk, 0:n], lhsT=self.kT_t[sl][pb:pb + 64, kb * 128:(kb + 1) * 128],
                                rhs=self.qT_t[qs][pb:pb + 64, qlo:qhi], start=True, stop=False)]
                    reads = [self.kvb[sl][kt], self.qTb[qs], self.constb]
                    if kind == "A":
                        off = 512 * tt + qlo - 128 * kb
                        mms.append(dict(out=self.ps_t[:, sbk, 0:n], lhsT=self.ident_t[:], rhs=self.BT_t[sl][:, e, off:off + n],
                                        start=False, stop=True))
                        reads.append(self.BTb[sl])
                    else:
                        diag = kb >= 4 * tt
                        mms.append(dict(out=self.ps_t[:, sbk, 0:n], lhsT=self.esel_t[:, h * 128:(h + 1) * 128],
                                        rhs=self.c1_t[qs][:, qlo:qhi], start=False, stop=not diag))
                        if diag:
                            mms.append(dict(out=self.ps_t[:, sbk, 0:128], lhsT=self.ident_t[:], rhs=self.mask_t[:],
                                            start=False, stop=True))
                        reads.append(self.c1b[qs])
                    self.mm(mms, reads, [self.ps[sbk]])
                    if kind == "A":
                        self.op(self.act, lambda: nc.scalar.activation(out=self.pT_t[sbk][:, 0:n], in_=self.ps_t[:, sbk, 0:n], func=AF.Exp),
                                [self.ps[sbk]], [self.pTb[sbk]])
                    else:
                        self.op(self.act, lambda: nc.scalar.activation(out=self.pT_t[sbk][:, 0:n], in_=self.ps_t[:, sbk, 0:n], func=AF.Exp,
                                                                       bias=self.negc_t[:, kb, h:h + 1]),
                                [self.ps[sbk], self.negcb], [self.pTb[sbk]])

                    def pv_fn(sl=sl, kb=kb, kt=kt, e=e, pb=pb, sbk=sbk, n=n, qlo=qlo, qhi=qhi, pvb=pvb, first=(ui == 0),
                              last=(ui == nu - 1), os_=os_, hp=hp, tt=tt):
                        self.mm([dict(out=self.ps_t[:, pvb, qlo:qhi], lhsT=self.V_t[sl][:, kb, 64 * e:64 * e + 128],
                                      rhs=self.pT_t[sbk][:, 0:n], start=first, stop=last)],
                                [self.kvb[sl][kt], self.pTb[sbk]], [self.ps[pvb]])
                        if last:
                            rk = pvb - 4
                            dlo = 64 - pb
                            self.op(self.dve, lambda: nc.vector.reciprocal(out=self.rec_t[rk][pb:pb + 64, :], in_=self.ps_t[dlo:dlo + 64, pvb, :]),
                                    [self.ps[pvb]], [self.recb[rk]])
                            self.op(self.dve, lambda: nc.vector.tensor_tensor(out=self.oT_t[os_][pb:pb + 64, :], in0=self.ps_t[pb:pb + 64, pvb, :],
                                                                              in1=self.rec_t[rk][pb:pb + 64, :], op=ALU.mult),
                                    [self.ps[pvb], self.recb[rk]], [self.oTb[os_]])
                            if e == 1:
                                self.dma(self.sp, self.oD[:, hp, self.tsl(tt)], self.oT_t[os_][:], [self.oTb[os_]], [self.oDb[tt]], self.oTo[os_])

                    pending.append(pv_fn)
                    flush(LOOKAHEAD)
        flush(0)
        self.alias_barrier(self.att_view, self.ffn_view)

    def build(self, debug_stop=None):
        self.setup()
        sets = []
        for L in range(DEPTH):
            for si in range(3):
                sets.append(("ffn", (L, 0, si)))
            if L < N_A:
                sets.append(("projA", L))
                sets.append(("oA", L))
            else:
                sets.append(("projB", L - N_A))
                sets.append(("oB", L - N_A))
            for si in range(3):
                sets.append(("ffn", (L, 1, si)))
            if L == N_A - 1:
                sets.append(("kv", None))

        def wspec(i):
            kind, a = sets[i]
            if kind == "ffn":
                return self.wset_ffn(*a)
            if kind == "projA":
                return self.wset_mat(self.wqkv_in[a], 3072)
            if kind == "oA":
                return self.wset_mat(self.awo_in[a], 1024)
            if kind == "projB":
                return self.wset_mat(self.bwq_in[a], 1024)
            if kind == "oB":
                return self.wset_mat(self.bwo_in[a], 1024)
            return self.wset_mat(self.wkvf_in, 2 * D + NH)

        self.load_wset(wspec(0), 0)
        first_src = True
        for i, (kind, a) in enumerate(sets):
            slot = i % 2
            if i + 1 < len(sets) and not (debug_stop is not None and i >= abs(debug_stop)):
                self.load_wset(wspec(i + 1), (i + 1) % 2)
            if kind == "ffn":
                L, pos, si = a
                src, srcb = (self.x_in, self.xb) if first_src else (self.hD, self.hDb)
                first_src = False
                self.ffn_pass(slot, si, L * 2 + pos, src, srcb, first=(si == 0))
            elif kind == "projA":
                self.proj_pass(slot, "A", 8 + a, 3072)
                self.attn_pass("A", a)
            elif kind == "projB":
                self.proj_pass(slot, "B", 8 + N_A + a, 1024)
                self.attn_pass("B", a)
            elif kind in ("oA", "oB"):
                self.out_proj_pass(slot)
            else:
                self.proj_pass(slot, "KV", 12, 2 * D + NH)
            if debug_stop is not None and i == abs(debug_stop):
                break
        if debug_stop is not None and debug_stop < 0:
            nc = self.nc
            for t in range(self.NT):
                s_ = t % 2
                self.load_hn(self.oD, self.oDb, t, s_)
                for c in range(DC):
                    self.op(self.dve, lambda: nc.vector.tensor_copy(out=self.hT_t[s_][:, c, :], in_=self.hn_t[s_][:, c, :]),
                            [self.hnb[s_][c]], [self.hTb[s_][c]])
                self.store_h(self.out, self.outb, t, s_)
        else:
            self.final_pass(do_norm=(debug_stop is None))
        for o in self.hTo:
            self.sp.wait(o.dsem, o.dcnt)
        return self.nc


def _host_inputs(inputs, S):
    f32 = np.float32
    x = np.asarray(inputs["x"], f32)
    B = x.shape[0]
    gains = np.concatenate([
        np.asarray(inputs["ffn_norm"], f32).reshape(8, D),
        np.asarray(inputs["mix_norm"], f32).reshape(4, D),
        np.asarray(inputs["kv_norm"], f32).reshape(1, D),
        np.asarray(inputs["final_norm"], f32).reshape(1, D)], axis=0)
    gains = np.ascontiguousarray(gains.reshape(14, DC, 128).transpose(2, 0, 1).reshape(128, 14 * DC))
    rb = np.asarray(inputs["a_rel_bias"], f32)
    sp = np.arange(128)[:, None]
    tp = np.arange(640)[None, :]
    idx = np.clip(tp - sp, -256, 256) + 256
    bt = rb[:, :, idx]
    bt = np.ascontiguousarray(bt.reshape(N_A, 8, 2, 128, 640).transpose(0, 1, 3, 2, 4))
    bf_bc = np.ascontiguousarray(np.broadcast_to(np.asarray(inputs["b_f_bias"], f32)[None, :], (128, NH)))
    ident = np.eye(128, dtype=f32)
    ones = np.ones((128, 128), f32)
    r = np.arange(128)
    mask = np.where(r[:, None] <= r[None, :], 0.0, NEG).astype(f32)
    tri = (r[:, None] <= r[None, :]).astype(f32)
    sel = np.zeros((128, 128), f32)
    sel[127, :] = 1.0
    esel = np.zeros((NH, NH, 128), f32)
    for h in range(NH):
        esel[h, h, :] = 1.0
    common = dict(
        gains=gains,
        ffn_w_gate=np.asarray(inputs["ffn_w_gate"], f32), ffn_w_up=np.asarray(inputs["ffn_w_up"], f32),
        ffn_w_down=np.asarray(inputs["ffn_w_down"], f32), a_w_qkv=np.asarray(inputs["a_w_qkv"], f32),
        a_w_o=np.asarray(inputs["a_w_o"], f32), bt_full=bt, b_w_kvf=np.asarray(inputs["b_w_kvf"], f32),
        bf_bc=bf_bc, b_w_q=np.asarray(inputs["b_w_q"], f32), b_w_o=np.asarray(inputs["b_w_o"], f32),
        c_ident=ident, c_ones=ones, c_mask=mask, c_tri=tri, c_sel=sel, c_esel=esel.reshape(NH, NH * 128))
    maps = []
    for b in range(B):
        xT = np.ascontiguousarray(x[b].reshape(S, DC, 128).transpose(2, 1, 0))
        m = dict(common)
        m["xT"] = xT
        maps.append(m)
    return maps


_CACHE = {}


def run(inputs, S, n_cores, debug_stop=None):
    key = (S, debug_stop)
    if key not in _CACHE:
        _CACHE[key] = Prog(S).build(debug_stop)
    nc = _CACHE[key]
    maps = _host_inputs(inputs, S)[:n_cores]
    res = run_bass_kernel_spmd(nc, maps, core_ids=list(range(n_cores)))
    outs = []
    for r in res.results:
        oT = np.asarray(r["outT"])
        outs.append(oT.transpose(2, 1, 0).reshape(S, D))
    return np.stack(outs, axis=0).astype(np.float32)


def kernel(**inputs):
    return run(inputs, SEQ, 8)
```

```python
import numpy as np
from contextlib import ExitStack
import concourse.bass as bass
import concourse.mybir as mybir
from concourse.bass_utils import run_bass_kernel_spmd

F32 = mybir.dt.float32
BF16 = mybir.dt.bfloat16
AF = mybir.ActivationFunctionType
ALU = mybir.AluOpType

D = 1024
DC = 8
FF = 2816
NH = 16
DEPTH = 4
N_A = 2
EPS = 1e-6
SEQ = 4096
SLICES = [(0, 8), (8, 15), (15, 22)]
NEG = -30000.0
SB_BASE = 16512
SB_TOP = 229344
LOOKAHEAD = 2


def _merge(d, sem, val):
    k = id(sem)
    if k not in d or d[k][1] < val:
        d[k] = (sem, val)


class Buf:
    def __init__(self, name):
        self.name = name
        self.w = {}
        self.r = {}


class Owner:
    def __init__(self, sem):
        self.dsem = sem
        self.dcnt = 0


class Eng:
    def __init__(self, name, e, sem, is_pe=False):
        self.name = name
        self.e = e
        self.sem = sem
        self.n = 0
        self.seen = {}
        self.is_pe = is_pe

    def wait(self, sem, val):
        if self.is_pe and sem is self.sem:
            return
        k = id(sem)
        if self.seen.get(k, 0) >= val:
            return
        self.e.wait_ge(sem, val)
        self.seen[k] = val


class Prog:
    def __init__(self, S):
        self.S = S
        self.NT = S // 512
        self.NB = S // 128
        self.nc = bass.Bass("TRN2", target_bir_lowering=False)
        self.stack = ExitStack()
        self.sb_off = SB_BASE
        self.nsem = 0
        nc = self.nc
        self.pe = Eng("pe", nc.tensor, self.sem("pe"), is_pe=True)
        self.act = Eng("act", nc.scalar, self.sem("act"))
        self.dve = Eng("dve", nc.vector, self.sem("dve"))
        self.pool = Eng("pool", nc.gpsimd, self.sem("pool"))
        self.sp = Eng("sp", nc.sync, self.sem("sp"))

    def sem(self, name):
        self.nsem += 1
        return self.stack.enter_context(self.nc.semaphore("s_" + name))

    def owner(self, name):
        return Owner(self.sem(name))

    def sb(self, name, shape, dt):
        n = 1
        for s in shape[1:]:
            n *= s
        nbytes = n * (4 if dt == F32 else 2)
        off = self.sb_off
        self.sb_off += (nbytes + 31) // 32 * 32
        assert self.sb_off <= SB_TOP, (name, self.sb_off)
        return self.nc.alloc_sbuf_tensor_at(name, shape, dt, offset=off)

    def sb_at(self, name, shape, dt, off):
        return self.nc.alloc_sbuf_tensor_at(name, shape, dt, offset=off)

    def _waits(self, eng, reads, writes):
        for b in reads:
            for sem, val in b.w.values():
                eng.wait(sem, val)
        for b in writes:
            for sem, val in b.w.values():
                eng.wait(sem, val)
            for sem, val in b.r.values():
                eng.wait(sem, val)

    def _mark(self, sem, val, reads, writes):
        for b in reads:
            _merge(b.r, sem, val)
        for b in writes:
            _merge(b.w, sem, val)

    def op(self, eng, fn, reads=(), writes=()):
        self._waits(eng, reads, writes)
        ins = fn()
        eng.n += 1
        ins.then_inc(eng.sem, 1)
        self._mark(eng.sem, eng.n, reads, writes)

    def mm(self, mms, reads=(), writes=()):
        eng = self.pe
        self._waits(eng, reads, writes)
        ins = None
        for kw in mms:
            ins = self.nc.tensor.matmul(kw["out"], lhsT=kw["lhsT"], rhs=kw["rhs"],
                                        start=kw["start"], stop=kw["stop"])
        eng.n += 1
        ins.then_inc(eng.sem, 1)
        self._mark(eng.sem, eng.n, reads, writes)

    def dma(self, q, out, in_, reads, writes, owner):
        self._waits(q, reads, writes)
        ins = q.e.dma_start(out=out, in_=in_)
        owner.dcnt += 16
        ins.then_inc(owner.dsem, 16)
        self._mark(owner.dsem, owner.dcnt, reads, writes)

    def alias_barrier(self, old, new):
        deps = {}
        for b in old:
            for sem, val in list(b.w.values()) + list(b.r.values()):
                _merge(deps, sem, val)
        for b in new:
            for sem, val in deps.values():
                _merge(b.w, sem, val)

    def setup(self):
        nc, S, NT, NB = self.nc, self.S, self.NT, self.NB
        dt = nc.dram_tensor
        I = "ExternalInput"
        self.x_in = dt("xT", [128, DC, S], F32, kind=I).ap()
        self.gains_in = dt("gains", [128, 14 * DC], F32, kind=I).ap()
        self.wg_in = dt("ffn_w_gate", [DEPTH, 2, D, FF], F32, kind=I).ap()
        self.wu_in = dt("ffn_w_up", [DEPTH, 2, D, FF], F32, kind=I).ap()
        self.wd_in = dt("ffn_w_down", [DEPTH, 2, FF, D], F32, kind=I).ap()
        self.wqkv_in = dt("a_w_qkv", [N_A, D, 3 * D], F32, kind=I).ap()
        self.awo_in = dt("a_w_o", [N_A, D, D], F32, kind=I).ap()
        self.bt_in = dt("bt_full", [N_A, 8, 128, 2, 640], F32, kind=I).ap()
        self.wkvf_in = dt("b_w_kvf", [D, 2 * D + NH], F32, kind=I).ap()
        self.bf_in = dt("bf_bc", [128, NH], F32, kind=I).ap()
        self.bwq_in = dt("b_w_q", [2, D, D], F32, kind=I).ap()
        self.bwo_in = dt("b_w_o", [2, D, D], F32, kind=I).ap()
        self.c_ident_in = dt("c_ident", [128, 128], F32, kind=I).ap()
        self.c_ones_in = dt("c_ones", [128, 128], F32, kind=I).ap()
        self.c_mask_in = dt("c_mask", [128, 128], F32, kind=I).ap()
        self.c_tri_in = dt("c_tri", [128, 128], F32, kind=I).ap()
        self.c_sel_in = dt("c_sel", [128, 128], F32, kind=I).ap()
        self.c_esel_in = dt("c_esel", [NH, NH * 128], F32, kind=I).ap()
        self.out = dt("outT", [128, DC, S], F32, kind="ExternalOutput").ap()
        self.hD = dt("hD", [128, DC, S], F32).ap()
        self.hnD = dt("hnD", [128, DC, S], BF16).ap()
        self.qD = dt("qD", [8, 128, S], BF16).ap()
        self.kD = dt("kD", [8, 128, S], BF16).ap()
        self.vD = dt("vD", [8, NB, 128, 192], BF16).ap()
        self.oD = dt("oD", [128, DC, S], BF16).ap()
        self.c1D = dt("c1D", [NH, S], BF16).ap()
        mk = lambda n: [Buf(f"{n}{t}") for t in range(NT)]
        self.xb, self.hDb, self.hnDb, self.qDb, self.kDb, self.vDb, self.oDb, self.c1Db, self.outb = (
            mk("x"), mk("hD"), mk("hnD"), mk("qD"), mk("kD"), mk("vD"), mk("oD"), mk("c1D"), mk("out"))
        self.inb = Buf("inputs")

        self.ps_t = nc.alloc_psum_tensor("ps", [128, 8, 512], F32)
        self.ps = [Buf(f"ps{i}") for i in range(8)]

        self.W_t = [self.sb(f"W{i}", [128, 24576], BF16) for i in range(2)]
        self.Wb = [Buf(f"W{i}") for i in range(2)]
        self.Wo_ = [self.owner(f"W{i}") for i in range(2)]
        arena = self.sb_off
        self.hT_t = [self.sb(f"hT{i}", [128, DC, 512], F32) for i in range(2)]
        self.hn_t = [self.sb(f"hn{i}", [128, DC, 512], BF16) for i in range(2)]
        arena_end = self.sb_off
        self.hTb = [[Buf(f"hT{i}_{c}") for c in range(DC)] for i in range(2)]
        self.hnb = [[Buf(f"hn{i}_{c}") for c in range(DC)] for i in range(2)]
        self.hTo = [self.owner(f"hT{i}") for i in range(2)]
        self.hno = [self.owner(f"hn{i}") for i in range(2)]
        kvbytes = S * 2 + NB * 192 * 2
        self.kT_t, self.V_t, self.kvb, self.kvo = [], [], [], []
        off = arena
        for i in range(2):
            self.kT_t.append(self.sb_at(f"kT{i}", [128, S], BF16, off))
            self.V_t.append(self.sb_at(f"V{i}", [128, NB, 192], BF16, off + S * 2))
            off += kvbytes
            self.kvb.append([Buf(f"kv{i}_{t}") for t in range(NT)])
            self.kvo.append([self.owner(f"kv{i}_{t}") for t in range(NT)])
        self.qT_t, self.BT_t, self.qTb, self.BTb, self.qTo, self.BTo = [], [], [], [], [], []
        for i in range(2):
            self.BT_t.append(self.sb_at(f"BT{i}", [128, 2, 640], BF16, off)); off += 2560
            self.BTb.append(Buf(f"BT{i}")); self.BTo.append(self.owner(f"BT{i}"))
        assert off <= arena_end, (off, arena_end)
        for i in range(2):
            self.qT_t.append([self.sb(f"qz{i}_{e}", [128, 512], BF16) for e in range(2)])
            self.qTb.append(Buf(f"qT{i}")); self.qTo.append(self.owner(f"qT{i}"))
        self.c1_t, self.c1b, self.c1o = [], [], []
        for i in range(2):
            self.c1_t.append(self.sb(f"c1t{i}", [128, 512], BF16))
            self.c1b.append(Buf(f"c1t{i}")); self.c1o.append(self.owner(f"c1t{i}"))
        self.ffn_view = [b for sl in self.hTb + self.hnb for b in sl]
        self.att_view = [b for sl in self.kvb for b in sl] + self.BTb

        self.sq_t = [self.sb(f"sq{i}", [128, 512], BF16) for i in range(4)]
        self.sqb = [Buf(f"sq{i}") for i in range(4)]
        self.act_t = self.sb("act", [128, 8, 512], BF16)
        self.actb = [Buf(f"act{i}") for i in range(8)]
        self.rstd_t = self.sb("rstd", [128, 512], F32)
        self.rstdb = Buf("rstd")
        self.silu_t = [self.sb(f"silu{i}", [128, 512], F32) for i in range(2)]
        self.silub = [Buf(f"silu{i}") for i in range(2)]
        self.g_t = self.sb("gains", [128, 14 * DC], F32)
        self.ident_t = self.sb("ident", [128, 128], BF16)
        self.ones_t = self.sb("ones", [128, 128], BF16)
        self.mask_t = self.sb("mask", [128, 128], BF16)
        self.esel_t = self.sb("esel", [128, NH * 128], BF16)
        self.identf_t = self.sb("identf", [128, 128], F32)
        self.tri_t = self.sb("tri", [128, 128], F32)
        self.sel_t = self.sb("sel", [128, 128], F32)
        self.bf_t = self.sb("bfbc", [128, NH], F32)
        self.constb = Buf("consts")
        self.consto = self.owner("consts")
        self.negc_t = self.sb("negc", [128, NB, NH], F32)
        self.negcb = Buf("negc")
        self.stg_t = [self.sb(f"stg{i}", [128, 512], BF16) for i in range(8)]
        self.stgb = [Buf(f"stg{i}") for i in range(8)]
        self.stgo = [self.owner(f"stg{i}") for i in range(8)]
        self.stg_i = 0
        self.vstg_t = [self.sb(f"vstg{i}", [128, 8, 192], BF16) for i in range(2)]
        self.vstgb = [Buf(f"vstg{i}") for i in range(2)]
        self.vstgo = [self.owner(f"vstg{i}") for i in range(2)]
        self.pT_t = [self.sb(f"pT{i}", [128, 512], BF16) for i in range(4)]
        self.pTb = [Buf(f"pT{i}") for i in range(4)]
        self.rec_t = [self.sb(f"rec{i}", [128, 512], F32) for i in range(2)]
        self.recb = [Buf(f"rec{i}") for i in range(2)]
        self.oT_t = [self.sb(f"oT{i}", [128, 512], BF16) for i in range(2)]
        self.oTb = [Buf(f"oT{i}") for i in range(2)]
        self.oTo = [self.owner(f"oT{i}") for i in range(2)]
        self.z_t = self.sb("ztmp", [128, NH], F32)
        self.zb = Buf("ztmp")
        self.l_t = self.sb("ltmp", [128, NH], F32)
        self.lb = Buf("ltmp")
        self.c1stg_t = self.sb("c1stg", [NH, 512], BF16)
        self.c1stgb = Buf("c1stg")
        self.c1stgo = self.owner("c1stg")
        self.bank_i = 0

        sp, pool = self.sp, self.pool
        cb, co = self.constb, self.consto
        self.dma(sp, self.g_t[:], self.gains_in, [self.inb], [cb], co)
        self.dma(sp, self.identf_t[:], self.c_ident_in, [self.inb], [cb], co)
        self.dma(sp, self.tri_t[:], self.c_tri_in, [self.inb], [cb], co)
        self.dma(sp, self.sel_t[:], self.c_sel_in, [self.inb], [cb], co)
        self.dma(sp, self.bf_t[:], self.bf_in, [self.inb], [cb], co)
        self.dma(pool, self.ident_t[:], self.c_ident_in, [self.inb], [cb], co)
        self.dma(pool, self.ones_t[:], self.c_ones_in, [self.inb], [cb], co)
        self.dma(pool, self.mask_t[:], self.c_mask_in, [self.inb], [cb], co)
        self.op(self.dve, lambda: nc.vector.memset(self.esel_t[:], 0.0), [], [cb])
        self.dma(pool, self.esel_t[0:NH, :], self.c_esel_in, [self.inb], [cb], co)
        for i in range(2):
            self.op(self.dve, lambda: nc.vector.memset(self.c1_t[i][:], 0.0), [], [self.c1b[i]])
            for e in range(2):
                self.op(self.dve, lambda: nc.vector.memset(self.qT_t[i][e][:], 0.0), [], [self.qTb[i]])
        for i in range(2):
            self.op(self.dve, lambda: nc.vector.memset(self.vstg_t[i][:, :, 64:128], 1.0), [], [self.vstgb[i]])

    def load_wset(self, ws, slot):
        W = self.W_t[slot]
        for (o, ncol, nch, src) in ws:
            dst = W[:, o:o + nch * ncol].rearrange("p (c f) -> p c f", c=nch)
            self.dma(self.pool, dst, src, [self.inb], [self.Wb[slot]], self.Wo_[slot])

    def wset_ffn(self, L, pos, si):
        f0, f1 = SLICES[si]
        nf = f1 - f0
        g = self.wg_in[L, pos].rearrange("(c p) f -> p c f", p=128)[:, :, f0 * 128:f1 * 128]
        u = self.wu_in[L, pos].rearrange("(c p) f -> p c f", p=128)[:, :, f0 * 128:f1 * 128]
        d = self.wd_in[L, pos].rearrange("(f p) d -> p f d", p=128)[:, f0:f1, :]
        return [(0, nf * 128, 8, g), (8192, nf * 128, 8, u), (16384, 1024, nf, d)]

    def wset_mat(self, src2d, ncols):
        return [(0, ncols, 8, src2d.rearrange("(c p) f -> p c f", p=128))]

    def Wview(self, slot, o, nch, ncol):
        return self.W_t[slot][:, o:o + nch * ncol].rearrange("p (c f) -> p c f", c=nch)

    def tsl(self, t):
        return slice(t * 512, (t + 1) * 512)

    def load_h(self, src, srcb, t, slot):
        self.dma(self.sp, self.hT_t[slot][:], src[:, :, self.tsl(t)], [srcb[t]], self.hTb[slot], self.hTo[slot])

    def load_hn(self, src, srcb, t, slot):
        self.dma(self.sp, self.hn_t[slot][:], src[:, :, self.tsl(t)], [srcb[t]], self.hnb[slot], self.hno[slot])

    def store_h(self, dst, dstb, t, slot):
        self.dma(self.sp, dst[:, :, self.tsl(t)], self.hT_t[slot][:], self.hTb[slot], [dstb[t]], self.hTo[slot])

    def norm(self, slot, gidx, final=False):
        nc = self.nc
        hT, hn = self.hT_t[slot], self.hn_t[slot]
        for c in range(DC):
            k = c % 4
            self.op(self.act, lambda: nc.scalar.activation(out=self.sq_t[k][:], in_=hT[:, c, :], func=AF.Square),
                    [self.hTb[slot][c]], [self.sqb[k]])
            self.mm([dict(out=self.ps_t[:, 7, :], lhsT=self.ones_t[:], rhs=self.sq_t[k][:], start=(c == 0), stop=(c == DC - 1))],
                    [self.sqb[k], self.constb], [self.ps[7]])
        self.op(self.act, lambda: nc.scalar.activation(out=self.rstd_t[:], in_=self.ps_t[:, 7, :], func=AF.Ln,
                                                       scale=1.0 / D, bias=EPS), [self.ps[7]], [self.rstdb])
        self.op(self.act, lambda: nc.scalar.activation(out=self.rstd_t[:], in_=self.rstd_t[:], func=AF.Exp, scale=-0.5),
                [self.rstdb], [self.rstdb])
        for c in range(DC):
            gc = self.g_t[:, gidx * DC + c:gidx * DC + c + 1]
            if final:
                self.op(self.dve, lambda: nc.vector.scalar_tensor_tensor(out=hT[:, c, :], in0=hT[:, c, :], scalar=gc,
                                                                         in1=self.rstd_t[:], op0=ALU.mult, op1=ALU.mult),
                        [self.hTb[slot][c], self.rstdb, self.constb], [self.hTb[slot][c]])
            else:
                self.op(self.dve, lambda: nc.vector.scalar_tensor_tensor(out=hn[:, c, :], in0=hT[:, c, :], scalar=gc,
                                                                         in1=self.rstd_t[:], op0=ALU.mult, op1=ALU.mult),
                        [self.hTb[slot][c], self.rstdb, self.constb], [self.hnb[slot][c]])

    def next_bank(self, n=6):
        b = self.bank_i % n
        self.bank_i += 1
        return b

    def ffn_pass(self, wslot, si, gidx, src, srcb, first):
        nc, NT = self.nc, self.NT
        f0, f1 = SLICES[si]
        nf = f1 - f0
        Wg = self.Wview(wslot, 0, 8, nf * 128)
        Wu = self.Wview(wslot, 8192, 8, nf * 128)
        Wd = self.Wview(wslot, 16384, nf, 1024)
        Wb = self.Wb[wslot]

        def prep(t):
            s = t % 2
            self.load_h(src, srcb, t, s)
            if not first:
                self.load_hn(self.hnD, self.hnDb, t, s)

        def do_norm(t):
            s = t % 2
            self.norm(s, gidx)
            self.dma(self.sp, self.hnD[:, :, self.tsl(t)], self.hn_t[s][:], self.hnb[s], [self.hnDb[t]], self.hno[s])

        prep(0)
        if first:
            do_norm(0)
        for t in range(NT):
            s = t % 2
            hn = self.hn_t[s]
            hT = self.hT_t[s]
            if t + 1 < NT:
                prep(t + 1)
            for fi in range(nf):
                gb, ub = fi % 2, 2 + fi % 2
                self.mm([dict(out=self.ps_t[:, gb, :], lhsT=Wg[:, c, fi * 128:(fi + 1) * 128], rhs=hn[:, c, :],
                              start=(c == 0), stop=(c == DC - 1)) for c in range(DC)],
                        self.hnb[s] + [Wb], [self.ps[gb]])
                self.mm([dict(out=self.ps_t[:, ub, :], lhsT=Wu[:, c, fi * 128:(fi + 1) * 128], rhs=hn[:, c, :],
                              start=(c == 0), stop=(c == DC - 1)) for c in range(DC)],
                        self.hnb[s] + [Wb], [self.ps[ub]])
                k = fi % 2
                self.op(self.act, lambda: nc.scalar.activation(out=self.silu_t[k][:], in_=self.ps_t[:, gb, :], func=AF.Silu),
                        [self.ps[gb]], [self.silub[k]])
                self.op(self.dve, lambda: nc.vector.tensor_tensor(out=self.act_t[:, fi, :], in0=self.ps_t[:, ub, :],
                                                                  in1=self.silu_t[k][:], op=ALU.mult),
                        [self.ps[ub], self.silub[k]], [self.actb[fi]])
            for dc in range(DC):
                if dc == 4 and first and t + 1 < NT:
                    do_norm(t + 1)
                db = 4 + dc % 2
                self.mm([dict(out=self.ps_t[:, db, :], lhsT=Wd[:, fi, dc * 128:(dc + 1) * 128], rhs=self.act_t[:, fi, :],
                              start=(fi == 0), stop=(fi == nf - 1)) for fi in range(nf)],
                        self.actb[:nf] + [Wb], [self.ps[db]])
                self.op(self.dve, lambda: nc.vector.scalar_tensor_tensor(out=hT[:, dc, :], in0=self.ps_t[:, db, :], scalar=0.5,
                                                                         in1=hT[:, dc, :], op0=ALU.mult, op1=ALU.add),
                        [self.ps[db], self.hTb[s][dc]], [self.hTb[s][dc]])
            self.store_h(self.hD, self.hDb, t, s)

    def evac_to_dram(self, bank, scale, dst_ap, dstb):
        nc = self.nc
        k = self.stg_i % 8
        self.stg_i += 1
        self.op(self.act, lambda: nc.scalar.activation(out=self.stg_t[k][:], in_=self.ps_t[:, bank, :], func=AF.Copy, scale=scale),
                [self.ps[bank]], [self.stgb[k]])
        self.dma(self.sp, dst_ap, self.stg_t[k][:], [self.stgb[k]], [dstb], self.stgo[k])

    def proj_fm(self, Wv, col0, s, t, scale, dstD, dstb):
        hn = self.hn_t[s]
        for oc in range(8):
            bank = self.next_bank()
            self.mm([dict(out=self.ps_t[:, bank, :], lhsT=Wv[:, c, col0 + oc * 128:col0 + (oc + 1) * 128], rhs=hn[:, c, :],
                          start=(c == 0), stop=(c == DC - 1)) for c in range(DC)],
                    self.hnb[s] + [self.Wb[self.cur_wslot]], [self.ps[bank]])
            self.evac_to_dram(bank, scale, dstD[oc, :, self.tsl(t)], dstb[t])

    def proj_v(self, Wv, col0, s, t):
        nc = self.nc
        hn = self.hn_t[s]
        for sub in range(4):
            blk = 4 * t + sub
            vs = blk % 2
            for hf in range(2):
                bank = self.next_bank()
                self.mm([dict(out=self.ps_t[:, bank, :], lhsT=hn[:, c, sub * 128:(sub + 1) * 128],
                              rhs=Wv[:, c, col0 + hf * 512:col0 + (hf + 1) * 512], start=(c == 0), stop=(c == DC - 1))
                         for c in range(DC)],
                        self.hnb[s] + [self.Wb[self.cur_wslot]], [self.ps[bank]])
                pv = self.ps_t[:, bank, :].rearrange("p (j e d) -> p j e d", j=4, e=2)
                for e in range(2):
                    self.op(self.dve, lambda: nc.vector.tensor_copy(out=self.vstg_t[vs][:, 4 * hf:4 * hf + 4, 128 * e:128 * e + 64],
                                                                    in_=pv[:, :, e, :]),
                            [self.ps[bank]], [self.vstgb[vs]])
            self.dma(self.sp, self.vD[:, blk].rearrange("h p c -> p h c"), self.vstg_t[vs][:], [self.vstgb[vs]],
                     [self.vDb[t]], self.vstgo[vs])

    def proj_f(self, Wv, s, t):
        nc = self.nc
        hn = self.hn_t[s]
        for sub in range(4):
            blk = 4 * t + sub
            self.mm([dict(out=self.ps_t[:, 6, 0:NH], lhsT=hn[:, c, sub * 128:(sub + 1) * 128], rhs=Wv[:, c, 2048:2048 + NH],
                          start=(c == 0), stop=(c == DC - 1)) for c in range(DC)],
                    self.hnb[s] + [self.Wb[self.cur_wslot]], [self.ps[6]])
            self.op(self.dve, lambda: nc.vector.tensor_tensor(out=self.z_t[:], in0=self.ps_t[:, 6, 0:NH], in1=self.bf_t[:], op=ALU.add),
                    [self.ps[6], self.constb], [self.zb])
            self.op(self.act, lambda: nc.scalar.activation(out=self.z_t[:], in_=self.z_t[:], func=AF.Exp, scale=-1.0),
                    [self.zb], [self.zb])
            self.op(self.act, lambda: nc.scalar.activation(out=self.l_t[:], in_=self.z_t[:], func=AF.Ln, bias=1.0),
                    [self.zb], [self.lb])
            mms = [dict(out=self.ps_t[:, 6, 16:16 + NH], lhsT=self.tri_t[:], rhs=self.l_t[:], start=True, stop=(blk == 0))]
            if blk > 0:
                mms.append(dict(out=self.ps_t[:, 6, 16:16 + NH], lhsT=self.sel_t[:], rhs=self.negc_t[:, blk - 1, :], start=False, stop=True))
            self.mm(mms, [self.lb, self.negcb, self.constb], [self.ps[6]])
            self.op(self.dve, lambda: nc.vector.tensor_copy(out=self.negc_t[:, blk, :], in_=self.ps_t[:, 6, 16:16 + NH]),
                    [self.ps[6]], [self.negcb])
            self.mm([dict(out=self.ps_t[0:NH, 6, 32:160], lhsT=self.negc_t[:, blk, :], rhs=self.identf_t[:], start=True, stop=True)],
                    [self.negcb, self.constb], [self.ps[6]])
            self.op(self.dve, lambda: nc.vector.tensor_scalar(out=self.c1stg_t[:, sub * 128:(sub + 1) * 128], in0=self.ps_t[0:NH, 6, 32:160],
                                                              scalar1=-1.0, scalar2=None, op0=ALU.mult),
                    [self.ps[6]], [self.c1stgb])
        self.dma(self.sp, self.c1D[:, self.tsl(t)], self.c1stg_t[:], [self.c1stgb], [self.c1Db[t]], self.c1stgo)

    def proj_pass(self, wslot, kind, gidx, ncols):
        NT = self.NT
        self.cur_wslot = wslot
        Wv = self.Wview(wslot, 0, 8, ncols)
        self.load_h(self.hD, self.hDb, 0, 0)
        self.norm(0, gidx)
        for t in range(NT):
            s = t % 2
            if t + 1 < NT:
                self.load_h(self.hD, self.hDb, t + 1, (t + 1) % 2)
            if kind == "A":
                self.proj_fm(Wv, 0, s, t, 0.125, self.qD, self.qDb)
                if t + 1 < NT:
                    self.norm((t + 1) % 2, gidx)
                self.proj_fm(Wv, 1024, s, t, 1.0, self.kD, self.kDb)
                self.proj_v(Wv, 2048, s, t)
            elif kind == "B":
                self.proj_fm(Wv, 0, s, t, 0.125, self.qD, self.qDb)
                if t + 1 < NT:
                    self.norm((t + 1) % 2, gidx)
            else:
                self.proj_fm(Wv, 0, s, t, 1.0, self.kD, self.kDb)
                if t + 1 < NT:
                    self.norm((t + 1) % 2, gidx)
                self.proj_v(Wv, 1024, s, t)
                self.proj_f(Wv, s, t)

    def out_proj_pass(self, wslot):
        nc, NT = self.nc, self.NT
        Wv = self.Wview(wslot, 0, 8, 1024)
        Wb = self.Wb[wslot]
        self.load_h(self.hD, self.hDb, 0, 0)
        self.load_hn(self.oD, self.oDb, 0, 0)
        for t in range(NT):
            s = t % 2
            if t + 1 < NT:
                self.load_h(self.hD, self.hDb, t + 1, (t + 1) % 2)
                self.load_hn(self.oD, self.oDb, t + 1, (t + 1) % 2)
            hT, o = self.hT_t[s], self.hn_t[s]
            for dc in range(DC):
                bank = self.next_bank()
                self.mm([dict(out=self.ps_t[:, bank, :], lhsT=Wv[:, c, dc * 128:(dc + 1) * 128], rhs=o[:, c, :],
                              start=(c == 0), stop=(c == DC - 1)) for c in range(DC)],
                        self.hnb[s] + [Wb], [self.ps[bank]])
                self.op(self.dve, lambda: nc.vector.tensor_tensor(out=hT[:, dc, :], in0=self.ps_t[:, bank, :], in1=hT[:, dc, :], op=ALU.add),
                        [self.ps[bank], self.hTb[s][dc]], [self.hTb[s][dc]])
            self.store_h(self.hD, self.hDb, t, s)

    def final_pass(self, do_norm=True):
        NT = self.NT
        self.load_h(self.hD, self.hDb, 0, 0)
        for t in range(NT):
            s = t % 2
            if t + 1 < NT:
                self.load_h(self.hD, self.hDb, t + 1, (t + 1) % 2)
            if do_norm:
                self.norm(s, 13, final=True)
            self.store_h(self.out, self.outb, t, s)

    def attn_pass(self, kind, la):
        nc, NT = self.nc, self.NT
        self.alias_barrier(self.ffn_view, self.att_view)
        pending = []
        state = dict(unit=0, pv=0)

        def load_kv(hp):
            sl = hp % 2
            for kt in range(NT):
                b, o = self.kvb[sl][kt], self.kvo[sl][kt]
                self.dma(self.sp, self.kT_t[sl][:, self.tsl(kt)], self.kD[hp, :, self.tsl(kt)], [self.kDb[kt]], [b], o)
                self.dma(self.sp, self.V_t[sl][:, 4 * kt:4 * kt + 4, :], self.vD[hp, 4 * kt:4 * kt + 4].rearrange("b p c -> p b c"),
                         [self.vDb[kt]], [b], o)
            if kind == "A":
                self.dma(self.pool, self.BT_t[sl][:], self.bt_in[la, hp], [self.inb], [self.BTb[sl]], self.BTo[sl])
                self.op(self.pool, lambda: nc.gpsimd.memset(self.BT_t[sl][0:64, :, 576:640], NEG), [], [self.BTb[sl]])
                self.op(self.pool, lambda: nc.gpsimd.memset(self.BT_t[sl][64:128, :, 0:64], NEG), [], [self.BTb[sl]])

        seq = [(hp, tt) for hp in range(8) for tt in range(NT)]

        def load_q(i):
            hp, tt = seq[i]
            qs = i % 2
            for e in range(2):
                self.dma(self.sp, self.qT_t[qs][e][64 * e:64 * e + 64, :], self.qD[hp, 64 * e:64 * e + 64, self.tsl(tt)],
                         [self.qDb[tt]], [self.qTb[qs]], self.qTo[qs])
            if kind == "B":
                self.dma(self.sp, self.c1_t[qs][0:NH, :], self.c1D[:, self.tsl(tt)], [self.c1Db[tt]], [self.c1b[qs]], self.c1o[qs])

        def flush(keep):
            while len(pending) > keep:
                pending.pop(0)()

        load_kv(0)
        load_q(0)
        for i, (hp, tt) in enumerate(seq):
            sl, qs = hp % 2, i % 2
            if i + 1 < len(seq):
                load_q(i + 1)
            if tt == 1 and hp + 1 < 8:
                load_kv(hp + 1)
            os_ = i % 2
            for e in range(2):
                h = 2 * hp + e
                pb = 64 * e
                if kind == "A":
                    units = []
                    order = [3, 0, 1, 2, 4, 5, 6, 7] if tt > 0 else [4, 5, 6, 7]
                    for j in order:
                        kb = 4 * tt - 4 + j
                        if j <= 3:
                            units.append((kb, 0, 128 * (j + 1)))
                        else:
                            units.append((kb, 128 * (j - 4), 512))
                else:
                    units = [(kb, 0, 512) for kb in range(4 * tt)] + [(4 * tt + j, 128 * j, 512) for j in range(4)]
                pvb = 4 + state["pv"] % 2
                state["pv"] += 1
                nu = len(units)
                for ui, (kb, qlo, qhi) in enumerate(units):
                    n = qhi - qlo
                    u = state["unit"]
                    state["unit"] += 1
                    sbk = u % 4
                    kt = kb // 4
                    mms = [dict(out=self.ps_t[:, sbk, 0:n], lhsT=self.kT_t[sl][:, kb * 128:(kb + 1) * 128],
                                rhs=self.qT_t[qs][e][:, qlo:qhi], start=True, stop=False)]
                    reads = [self.kvb[sl][kt], self.qTb[qs], self.constb]
                    if kind == "A":
                        off = 512 * tt + qlo - 128 * kb
                        mms.append(dict(out=self.ps_t[:, sbk, 0:n], lhsT=self.ident_t[:], rhs=self.BT_t[sl][:, e, off:off + n],
                                        start=False, stop=True))
                        reads.append(self.BTb[sl])
                    else:
                        diag = kb >= 4 * tt
                        mms.append(dict(out=self.ps_t[:, sbk, 0:n], lhsT=self.esel_t[:, h * 128:(h + 1) * 128],
                                        rhs=self.c1_t[qs][:, qlo:qhi], start=False, stop=not diag))
                        if diag:
                            mms.append(dict(out=self.ps_t[:, sbk, 0:128], lhsT=self.ident_t[:], rhs=self.mask_t[:],
                                            start=False, stop=True))
                        reads.append(self.c1b[qs])
                    self.mm(mms, reads, [self.ps[sbk]])
                    if kind == "A":
                        self.op(self.act, lambda: nc.scalar.activation(out=self.pT_t[sbk][:, 0:n], in_=self.ps_t[:, sbk, 0:n], func=AF.Exp),
                                [self.ps[sbk]], [self.pTb[sbk]])
                    else:
                        self.op(self.act, lambda: nc.scalar.activation(out=self.pT_t[sbk][:, 0:n], in_=self.ps_t[:, sbk, 0:n], func=AF.Exp,
                                                                       bias=self.negc_t[:, kb, h:h + 1]),
                                [self.ps[sbk], self.negcb], [self.pTb[sbk]])

                    def pv_fn(sl=sl, kb=kb, kt=kt, e=e, pb=pb, sbk=sbk, n=n, qlo=qlo, qhi=qhi, pvb=pvb, first=(ui == 0),
                              last=(ui == nu - 1), os_=os_, hp=hp, tt=tt):
                        self.mm([dict(out=self.ps_t[:, pvb, qlo:qhi], lhsT=self.V_t[sl][:, kb, 64 * e:64 * e + 128],
                                      rhs=self.pT_t[sbk][:, 0:n], start=first, stop=last)],
                                [self.kvb[sl][kt], self.pTb[sbk]], [self.ps[pvb]])
                        if last:
                            rk = pvb - 4
                            dlo = 64 - pb
                            self.op(self.dve, lambda: nc.vector.reciprocal(out=self.rec_t[rk][pb:pb + 64, :], in_=self.ps_t[dlo:dlo + 64, pvb, :]),
                                    [self.ps[pvb]], [self.recb[rk]])
                            self.op(self.dve, lambda: nc.vector.tensor_tensor(out=self.oT_t[os_][pb:pb + 64, :], in0=self.ps_t[pb:pb + 64, pvb, :],
                                                                              in1=self.rec_t[rk][pb:pb + 64, :], op=ALU.mult),
                                    [self.ps[pvb], self.recb[rk]], [self.oTb[os_]])
                            if e == 1:
                                self.dma(self.sp, self.oD[:, hp, self.tsl(tt)], self.oT_t[os_][:], [self.oTb[os_]], [self.oDb[tt]], self.oTo[os_])

                    pending.append(pv_fn)
                    flush(LOOKAHEAD)
        flush(0)
        self.alias_barrier(self.att_view, self.ffn_view)

    def build(self, debug_stop=None):
        self.setup()
        sets = []
        for L in range(DEPTH):
            for si in range(3):
                sets.append(("ffn", (L, 0, si)))
            if L < N_A:
                sets.append(("projA", L))
                sets.append(("oA", L))
            else:
                sets.append(("projB", L - N_A))
                sets.append(("oB", L - N_A))
            for si in range(3):
                sets.append(("ffn", (L, 1, si)))
            if L == N_A - 1:
                sets.append(("kv", None))

        def wspec(i):
            kind, a = sets[i]
            if kind == "ffn":
                return self.wset_ffn(*a)
            if kind == "projA":
                return self.wset_mat(self.wqkv_in[a], 3072)
            if kind == "oA":
                return self.wset_mat(self.awo_in[a], 1024)
            if kind == "projB":
                return self.wset_mat(self.bwq_in[a], 1024)
            if kind == "oB":
                return self.wset_mat(self.bwo_in[a], 1024)
            return self.wset_mat(self.wkvf_in, 2 * D + NH)

        self.load_wset(wspec(0), 0)
        first_src = True
        for i, (kind, a) in enumerate(sets):
            slot = i % 2
            if i + 1 < len(sets) and not (debug_stop is not None and i >= abs(debug_stop)):
                self.load_wset(wspec(i + 1), (i + 1) % 2)
            if kind == "ffn":
                L, pos, si = a
                src, srcb = (self.x_in, self.xb) if first_src else (self.hD, self.hDb)
                first_src = False
                self.ffn_pass(slot, si, L * 2 + pos, src, srcb, first=(si == 0))
            elif kind == "projA":
                self.proj_pass(slot, "A", 8 + a, 3072)
                self.attn_pass("A", a)
            elif kind == "projB":
                self.proj_pass(slot, "B", 8 + N_A + a, 1024)
                self.attn_pass("B", a)
            elif kind in ("oA", "oB"):
                self.out_proj_pass(slot)
            else:
                self.proj_pass(slot, "KV", 12, 2 * D + NH)
            if debug_stop is not None and i == abs(debug_stop):
                break
        if debug_stop is not None and debug_stop < 0:
            nc = self.nc
            for t in range(self.NT):
                s_ = t % 2
                self.load_hn(self.oD, self.oDb, t, s_)
                for c in range(DC):
                    self.op(self.dve, lambda: nc.vector.tensor_copy(out=self.hT_t[s_][:, c, :], in_=self.hn_t[s_][:, c, :]),
                            [self.hnb[s_][c]], [self.hTb[s_][c]])
                self.store_h(self.out, self.outb, t, s_)
        else:
            self.final_pass(do_norm=(debug_stop is None))
        for o in self.hTo:
            self.sp.wait(o.dsem, o.dcnt)
        return self.nc


def _host_inputs(inputs, S):
    f32 = np.float32
    x = np.asarray(inputs["x"], f32)
    B = x.shape[0]
    gains = np.concatenate([
        np.asarray(inputs["ffn_norm"], f32).reshape(8, D),
        np.asarray(inputs["mix_norm"], f32).reshape(4, D),
        np.asarray(inputs["kv_norm"], f32).reshape(1, D),
        np.asarray(inputs["final_norm"], f32).reshape(1, D)], axis=0)
    gains = np.ascontiguousarray(gains.reshape(14, DC, 128).transpose(2, 0, 1).reshape(128, 14 * DC))
    rb = np.asarray(inputs["a_rel_bias"], f32)
    sp = np.arange(128)[:, None]
    tp = np.arange(640)[None, :]
    idx = np.clip(tp - sp, -256, 256) + 256
    bt = rb[:, :, idx]
    bt = np.ascontiguousarray(bt.reshape(N_A, 8, 2, 128, 640).transpose(0, 1, 3, 2, 4))
    bf_bc = np.ascontiguousarray(np.broadcast_to(np.asarray(inputs["b_f_bias"], f32)[None, :], (128, NH)))
    ident = np.eye(128, dtype=f32)
    ones = np.ones((128, 128), f32)
    r = np.arange(128)
    mask = np.where(r[:, None] <= r[None, :], 0.0, NEG).astype(f32)
    tri = (r[:, None] <= r[None, :]).astype(f32)
    sel = np.zeros((128, 128), f32)
    sel[127, :] = 1.0
    esel = np.zeros((NH, NH, 128), f32)
    for h in range(NH):
        esel[h, h, :] = 1.0
    common = dict(
        gains=gains,
        ffn_w_gate=np.asarray(inputs["ffn_w_gate"], f32), ffn_w_up=np.asarray(inputs["ffn_w_up"], f32),
        ffn_w_down=np.asarray(inputs["ffn_w_down"], f32), a_w_qkv=np.asarray(inputs["a_w_qkv"], f32),
        a_w_o=np.asarray(inputs["a_w_o"], f32), bt_full=bt, b_w_kvf=np.asarray(inputs["b_w_kvf"], f32),
        bf_bc=bf_bc, b_w_q=np.asarray(inputs["b_w_q"], f32), b_w_o=np.asarray(inputs["b_w_o"], f32),
        c_ident=ident, c_ones=ones, c_mask=mask, c_tri=tri, c_sel=sel, c_esel=esel.reshape(NH, NH * 128))
    maps = []
    for b in range(B):
        xT = np.ascontiguousarray(x[b].reshape(S, DC, 128).transpose(2, 1, 0))
        m = dict(common)
        m["xT"] = xT
        maps.append(m)
    return maps


_CACHE = {}


def run(inputs, S, n_cores, debug_stop=None):
    key = (S, debug_stop)
    if key not in _CACHE:
        _CACHE[key] = Prog(S).build(debug_stop)
    nc = _CACHE[key]
    maps = _host_inputs(inputs, S)[:n_cores]
    res = run_bass_kernel_spmd(nc, maps, core_ids=list(range(n_cores)))
    outs = []
    for r in res.results:
        oT = np.asarray(r["outT"])
        outs.append(oT.transpose(2, 1, 0).reshape(S, D))
    return np.stack(outs, axis=0).astype(np.float32)


def kernel(**inputs):
    return run(inputs, SEQ, 8)
```

```python
import numpy as np
from contextlib import ExitStack
import concourse.bass as bass
import concourse.mybir as mybir
from concourse.bass_utils import run_bass_kernel_spmd

F32 = mybir.dt.float32
BF16 = mybir.dt.bfloat16
AF = mybir.ActivationFunctionType
ALU = mybir.AluOpType

D = 1024
DC = 8
FF = 2816
NH = 16
DEPTH = 4
N_A = 2
EPS = 1e-6
SEQ = 4096
SLICES = [(0, 8), (8, 15), (15, 22)]
NEG = -30000.0
SB_BASE = 16512
SB_TOP = 229344
LOOKAHEAD = 2


def _merge(d, sem, val):
    k = id(sem)
    if k not in d or d[k][1] < val:
        d[k] = (sem, val)


class Buf:
    def __init__(self, name):
        self.name = name
        self.w = {}
        self.r = {}


class Owner:
    def __init__(self, sem):
        self.dsem = sem
        self.dcnt = 0


class Eng:
    def __init__(self, name, e, sem, is_pe=False):
        self.name = name
        self.e = e
        self.sem = sem
        self.n = 0
        self.seen = {}
        self.is_pe = is_pe

    def wait(self, sem, val):
        if self.is_pe and sem is self.sem:
            return
        k = id(sem)
        if self.seen.get(k, 0) >= val:
            return
        self.e.wait_ge(sem, val)
        self.seen[k] = val


class Prog:
    def __init__(self, S):
        self.S = S
        self.NT = S // 512
        self.NB = S // 128
        self.nc = bass.Bass("TRN2", target_bir_lowering=False)
        self.stack = ExitStack()
        self.sb_off = SB_BASE
        self.nsem = 0
        nc = self.nc
        self.pe = Eng("pe", nc.tensor, self.sem("pe"), is_pe=True)
        self.act = Eng("act", nc.scalar, self.sem("act"))
        self.dve = Eng("dve", nc.vector, self.sem("dve"))
        self.pool = Eng("pool", nc.gpsimd, self.sem("pool"))
        self.sp = Eng("sp", nc.sync, self.sem("sp"))

    def sem(self, name):
        self.nsem += 1
        return self.stack.enter_context(self.nc.semaphore("s_" + name))

    def owner(self, name):
        return Owner(self.sem(name))

    def sb(self, name, shape, dt):
        n = 1
        for s in shape[1:]:
            n *= s
        nbytes = n * (4 if dt == F32 else 2)
        off = self.sb_off
        self.sb_off += (nbytes + 31) // 32 * 32
        assert self.sb_off <= SB_TOP, (name, self.sb_off)
        return self.nc.alloc_sbuf_tensor_at(name, shape, dt, offset=off)

    def sb_at(self, name, shape, dt, off):
        return self.nc.alloc_sbuf_tensor_at(name, shape, dt, offset=off)

    def _waits(self, eng, reads, writes):
        for b in reads:
            for sem, val in b.w.values():
                eng.wait(sem, val)
        for b in writes:
            for sem, val in b.w.values():
                eng.wait(sem, val)
            for sem, val in b.r.values():
                eng.wait(sem, val)

    def _mark(self, sem, val, reads, writes):
        for b in reads:
            _merge(b.r, sem, val)
        for b in writes:
            _merge(b.w, sem, val)

    def op(self, eng, fn, reads=(), writes=()):
        self._waits(eng, reads, writes)
        ins = fn()
        eng.n += 1
        ins.then_inc(eng.sem, 1)
        self._mark(eng.sem, eng.n, reads, writes)

    def mm(self, mms, reads=(), writes=()):
        eng = self.pe
        self._waits(eng, reads, writes)
        ins = None
        for kw in mms:
            ins = self.nc.tensor.matmul(kw["out"], lhsT=kw["lhsT"], rhs=kw["rhs"],
                                        start=kw["start"], stop=kw["stop"])
        eng.n += 1
        ins.then_inc(eng.sem, 1)
        self._mark(eng.sem, eng.n, reads, writes)

    def dma(self, q, out, in_, reads, writes, owner):
        self._waits(q, reads, writes)
        ins = q.e.dma_start(out=out, in_=in_)
        owner.dcnt += 16
        ins.then_inc(owner.dsem, 16)
        self._mark(owner.dsem, owner.dcnt, reads, writes)

    def alias_barrier(self, old, new):
        deps = {}
        for b in old:
            for sem, val in list(b.w.values()) + list(b.r.values()):
                _merge(deps, sem, val)
        for b in new:
            for sem, val in deps.values():
                _merge(b.w, sem, val)

    def setup(self):
        nc, S, NT, NB = self.nc, self.S, self.NT, self.NB
        dt = nc.dram_tensor
        I = "ExternalInput"
        self.x_in = dt("xT", [128, DC, S], F32, kind=I).ap()
        self.gains_in = dt("gains", [128, 14 * DC], F32, kind=I).ap()
        self.wg_in = dt("ffn_w_gate", [DEPTH, 2, D, FF], F32, kind=I).ap()
        self.wu_in = dt("ffn_w_up", [DEPTH, 2, D, FF], F32, kind=I).ap()
        self.wd_in = dt("ffn_w_down", [DEPTH, 2, FF, D], F32, kind=I).ap()
        self.wqkv_in = dt("a_w_qkv", [N_A, D, 3 * D], F32, kind=I).ap()
        self.awo_in = dt("a_w_o", [N_A, D, D], F32, kind=I).ap()
        self.bt_in = dt("bt_full", [N_A, 8, 128, 2, 640], F32, kind=I).ap()
        self.wkvf_in = dt("b_w_kvf", [D, 2 * D + NH], F32, kind=I).ap()
        self.bf_in = dt("bf_bc", [128, NH], F32, kind=I).ap()
        self.bwq_in = dt("b_w_q", [2, D, D], F32, kind=I).ap()
        self.bwo_in = dt("b_w_o", [2, D, D], F32, kind=I).ap()
        self.c_ident_in = dt("c_ident", [128, 128], F32, kind=I).ap()
        self.c_ones_in = dt("c_ones", [128, 128], F32, kind=I).ap()
        self.c_mask_in = dt("c_mask", [128, 128], F32, kind=I).ap()
        self.c_tri_in = dt("c_tri", [128, 128], F32, kind=I).ap()
        self.c_sel_in = dt("c_sel", [128, 128], F32, kind=I).ap()
        self.c_esel_in = dt("c_esel", [NH, NH * 128], F32, kind=I).ap()
        self.out = dt("outT", [128, DC, S], F32, kind="ExternalOutput").ap()
        self.hD = dt("hD", [128, DC, S], F32).ap()
        self.hnD = dt("hnD", [128, DC, S], BF16).ap()
        self.qD = dt("qD", [8, 128, S], BF16).ap()
        self.kD = dt("kD", [8, 128, S], BF16).ap()
        self.vD = dt("vD", [8, NB, 128, 192], BF16).ap()
        self.oD = dt("oD", [128, DC, S], BF16).ap()
        self.c1D = dt("c1D", [NH, S], BF16).ap()
        mk = lambda n: [Buf(f"{n}{t}") for t in range(NT)]
        self.xb, self.hDb, self.hnDb, self.qDb, self.kDb, self.vDb, self.oDb, self.c1Db, self.outb = (
            mk("x"), mk("hD"), mk("hnD"), mk("qD"), mk("kD"), mk("vD"), mk("oD"), mk("c1D"), mk("out"))
        self.inb = Buf("inputs")

        self.ps_t = nc.alloc_psum_tensor("ps", [128, 8, 512], F32)
        self.ps = [Buf(f"ps{i}") for i in range(8)]

        self.W_t = [self.sb(f"W{i}", [128, 24576], BF16) for i in range(2)]
        self.Wb = [Buf(f"W{i}") for i in range(2)]
        self.Wo_ = [self.owner(f"W{i}") for i in range(2)]
        arena = self.sb_off
        self.hT_t = [self.sb(f"hT{i}", [128, DC, 512], F32) for i in range(2)]
        self.hn_t = [self.sb(f"hn{i}", [128, DC, 512], BF16) for i in range(2)]
        arena_end = self.sb_off
        self.hTb = [[Buf(f"hT{i}_{c}") for c in range(DC)] for i in range(2)]
        self.hnb = [[Buf(f"hn{i}_{c}") for c in range(DC)] for i in range(2)]
        self.hTo = [self.owner(f"hT{i}") for i in range(2)]
        self.hno = [self.owner(f"hn{i}") for i in range(2)]
        kvbytes = S * 2 + NB * 192 * 2
        self.kT_t, self.V_t, self.kvb, self.kvo = [], [], [], []
        off = arena
        for i in range(2):
            self.kT_t.append(self.sb_at(f"kT{i}", [128, S], BF16, off))
            self.V_t.append(self.sb_at(f"V{i}", [128, NB, 192], BF16, off + S * 2))
            off += kvbytes
            self.kvb.append([Buf(f"kv{i}_{t}") for t in range(NT)])
            self.kvo.append([self.owner(f"kv{i}_{t}") for t in range(NT)])
        self.qT_t, self.BT_t, self.qTb, self.BTb, self.qTo, self.BTo = [], [], [], [], [], []
        for i in range(2):
            self.BT_t.append(self.sb_at(f"BT{i}", [128, 2, 640], BF16, off)); off += 2560
            self.BTb.append(Buf(f"BT{i}")); self.BTo.append(self.owner(f"BT{i}"))
        assert off <= arena_end, (off, arena_end)
        for i in range(2):
            self.qT_t.append([self.sb(f"qz{i}_{e}", [128, 512], BF16) for e in range(2)])
            self.qTb.append(Buf(f"qT{i}")); self.qTo.append(self.owner(f"qT{i}"))
        self.c1_t, self.c1b, self.c1o = [], [], []
        for i in range(2):
            self.c1_t.append(self.sb(f"c1t{i}", [128, 512], BF16))
            self.c1b.append(Buf(f"c1t{i}")); self.c1o.append(self.owner(f"c1t{i}"))
        self.ffn_view = [b for sl in self.hTb + self.hnb for b in sl]
        self.att_view = [b for sl in self.kvb for b in sl] + self.BTb

        self.sq_t = [self.sb(f"sq{i}", [128, 512], BF16) for i in range(4)]
        self.sqb = [Buf(f"sq{i}") for i in range(4)]
        self.act_t = self.sb("act", [128, 8, 512], BF16)
        self.actb = [Buf(f"act{i}") for i in range(8)]
        self.rstd_t = self.sb("rstd", [128, 512], F32)
        self.rstdb = Buf("rstd")
        self.silu_t = [self.sb(f"silu{i}", [128, 512], F32) for i in range(2)]
        self.silub = [Buf(f"silu{i}") for i in range(2)]
        self.g_t = self.sb("gains", [128, 14 * DC], F32)
        self.ident_t = self.sb("ident", [128, 128], BF16)
        self.ones_t = self.sb("ones", [128, 128], BF16)
        self.mask_t = self.sb("mask", [128, 128], BF16)
        self.esel_t = self.sb("esel", [128, NH * 128], BF16)
        self.identf_t = self.sb("identf", [128, 128], F32)
        self.tri_t = self.sb("tri", [128, 128], F32)
        self.sel_t = self.sb("sel", [128, 128], F32)
        self.bf_t = self.sb("bfbc", [128, NH], F32)
        self.constb = Buf("consts")
        self.consto = self.owner("consts")
        self.negc_t = self.sb("negc", [128, NB, NH], F32)
        self.negcb = Buf("negc")
        self.stg_t = [self.sb(f"stg{i}", [128, 512], BF16) for i in range(8)]
        self.stgb = [Buf(f"stg{i}") for i in range(8)]
        self.stgo = [self.owner(f"stg{i}") for i in range(8)]
        self.stg_i = 0
        self.vstg_t = [self.sb(f"vstg{i}", [128, 8, 192], BF16) for i in range(2)]
        self.vstgb = [Buf(f"vstg{i}") for i in range(2)]
        self.vstgo = [self.owner(f"vstg{i}") for i in range(2)]
        self.pT_t = [self.sb(f"pT{i}", [128, 512], BF16) for i in range(4)]
        self.pTb = [Buf(f"pT{i}") for i in range(4)]
        self.rec_t = [self.sb(f"rec{i}", [128, 512], F32) for i in range(2)]
        self.recb = [Buf(f"rec{i}") for i in range(2)]
        self.oT_t = [self.sb(f"oT{i}", [128, 512], BF16) for i in range(2)]
        self.oTb = [Buf(f"oT{i}") for i in range(2)]
        self.oTo = [self.owner(f"oT{i}") for i in range(2)]
        self.z_t = self.sb("ztmp", [128, NH], F32)
        self.zb = Buf("ztmp")
        self.l_t = self.sb("ltmp", [128, NH], F32)
        self.lb = Buf("ltmp")
        self.c1stg_t = self.sb("c1stg", [NH, 512], BF16)
        self.c1stgb = Buf("c1stg")
        self.c1stgo = self.owner("c1stg")
        self.bank_i = 0
        self.hoist = None
        self.hoisted = False

        sp, pool = self.sp, self.pool
        cb, co = self.constb, self.consto
        self.dma(sp, self.g_t[:], self.gains_in, [self.inb], [cb], co)
        self.dma(sp, self.identf_t[:], self.c_ident_in, [self.inb], [cb], co)
        self.dma(sp, self.tri_t[:], self.c_tri_in, [self.inb], [cb], co)
        self.dma(sp, self.sel_t[:], self.c_sel_in, [self.inb], [cb], co)
        self.dma(sp, self.bf_t[:], self.bf_in, [self.inb], [cb], co)
        self.dma(pool, self.ident_t[:], self.c_ident_in, [self.inb], [cb], co)
        self.dma(pool, self.ones_t[:], self.c_ones_in, [self.inb], [cb], co)
        self.dma(pool, self.mask_t[:], self.c_mask_in, [self.inb], [cb], co)
        self.op(self.dve, lambda: nc.vector.memset(self.esel_t[:], 0.0), [], [cb])
        self.dma(pool, self.esel_t[0:NH, :], self.c_esel_in, [self.inb], [cb], co)
        for i in range(2):
            self.op(self.dve, lambda: nc.vector.memset(self.c1_t[i][:], 0.0), [], [self.c1b[i]])
            for e in range(2):
                self.op(self.dve, lambda: nc.vector.memset(self.qT_t[i][e][:], 0.0), [], [self.qTb[i]])
        for i in range(2):
            self.op(self.dve, lambda: nc.vector.memset(self.vstg_t[i][:, :, 64:128], 1.0), [], [self.vstgb[i]])

    def load_wset(self, ws, slot):
        W = self.W_t[slot]
        for (o, ncol, nch, src) in ws:
            dst = W[:, o:o + nch * ncol].rearrange("p (c f) -> p c f", c=nch)
            self.dma(self.pool, dst, src, [self.inb], [self.Wb[slot]], self.Wo_[slot])

    def wset_ffn(self, L, pos, si):
        f0, f1 = SLICES[si]
        nf = f1 - f0
        g = self.wg_in[L, pos].rearrange("(c p) f -> p c f", p=128)[:, :, f0 * 128:f1 * 128]
        u = self.wu_in[L, pos].rearrange("(c p) f -> p c f", p=128)[:, :, f0 * 128:f1 * 128]
        d = self.wd_in[L, pos].rearrange("(f p) d -> p f d", p=128)[:, f0:f1, :]
        return [(0, nf * 128, 8, g), (8192, nf * 128, 8, u), (16384, 1024, nf, d)]

    def wset_mat(self, src2d, ncols):
        return [(0, ncols, 8, src2d.rearrange("(c p) f -> p c f", p=128))]

    def Wview(self, slot, o, nch, ncol):
        return self.W_t[slot][:, o:o + nch * ncol].rearrange("p (c f) -> p c f", c=nch)

    def tsl(self, t):
        return slice(t * 512, (t + 1) * 512)

    def load_h(self, src, srcb, t, slot):
        self.dma(self.sp, self.hT_t[slot][:], src[:, :, self.tsl(t)], [srcb[t]], self.hTb[slot], self.hTo[slot])

    def load_hn(self, src, srcb, t, slot):
        self.dma(self.sp, self.hn_t[slot][:], src[:, :, self.tsl(t)], [srcb[t]], self.hnb[slot], self.hno[slot])

    def store_h(self, dst, dstb, t, slot):
        self.dma(self.sp, dst[:, :, self.tsl(t)], self.hT_t[slot][:], self.hTb[slot], [dstb[t]], self.hTo[slot])

    def norm(self, slot, gidx, final=False):
        nc = self.nc
        hT, hn = self.hT_t[slot], self.hn_t[slot]
        for c in range(DC):
            k = c % 4
            self.op(self.act, lambda: nc.scalar.activation(out=self.sq_t[k][:], in_=hT[:, c, :], func=AF.Square),
                    [self.hTb[slot][c]], [self.sqb[k]])
            self.mm([dict(out=self.ps_t[:, 7, :], lhsT=self.ones_t[:], rhs=self.sq_t[k][:], start=(c == 0), stop=(c == DC - 1))],
                    [self.sqb[k], self.constb], [self.ps[7]])
        self.op(self.act, lambda: nc.scalar.activation(out=self.rstd_t[:], in_=self.ps_t[:, 7, :], func=AF.Ln,
                                                       scale=1.0 / D, bias=EPS), [self.ps[7]], [self.rstdb])
        self.op(self.act, lambda: nc.scalar.activation(out=self.rstd_t[:], in_=self.rstd_t[:], func=AF.Exp, scale=-0.5),
                [self.rstdb], [self.rstdb])
        for c in range(DC):
            gc = self.g_t[:, gidx * DC + c:gidx * DC + c + 1]
            if final:
                self.op(self.dve, lambda: nc.vector.scalar_tensor_tensor(out=hT[:, c, :], in0=hT[:, c, :], scalar=gc,
                                                                         in1=self.rstd_t[:], op0=ALU.mult, op1=ALU.mult),
                        [self.hTb[slot][c], self.rstdb, self.constb], [self.hTb[slot][c]])
            else:
                self.op(self.dve, lambda: nc.vector.scalar_tensor_tensor(out=hn[:, c, :], in0=hT[:, c, :], scalar=gc,
                                                                         in1=self.rstd_t[:], op0=ALU.mult, op1=ALU.mult),
                        [self.hTb[slot][c], self.rstdb, self.constb], [self.hnb[slot][c]])

    def next_bank(self, n=6):
        b = self.bank_i % n
        self.bank_i += 1
        return b

    def ffn_pass(self, wslot, si, gidx, src, srcb, first):
        nc, NT = self.nc, self.NT
        f0, f1 = SLICES[si]
        nf = f1 - f0
        Wg = self.Wview(wslot, 0, 8, nf * 128)
        Wu = self.Wview(wslot, 8192, 8, nf * 128)
        Wd = self.Wview(wslot, 16384, nf, 1024)
        Wb = self.Wb[wslot]

        def prep(t):
            s = t % 2
            self.load_h(src, srcb, t, s)
            if not first:
                self.load_hn(self.hnD, self.hnDb, t, s)

        def do_norm(t):
            s = t % 2
            self.norm(s, gidx)
            self.dma(self.sp, self.hnD[:, :, self.tsl(t)], self.hn_t[s][:], self.hnb[s], [self.hnDb[t]], self.hno[s])

        if not self.hoisted:
            prep(0)
            if first:
                do_norm(0)
        self.hoisted = False
        for t in range(NT):
            s = t % 2
            hn = self.hn_t[s]
            hT = self.hT_t[s]
            if t + 1 < NT:
                prep(t + 1)
            elif self.hoist:
                self.hoist[0]()
            for fi in range(nf):
                gb, ub = fi % 2, 2 + fi % 2
                self.mm([dict(out=self.ps_t[:, gb, :], lhsT=Wg[:, c, fi * 128:(fi + 1) * 128], rhs=hn[:, c, :],
                              start=(c == 0), stop=(c == DC - 1)) for c in range(DC)],
                        self.hnb[s] + [Wb], [self.ps[gb]])
                self.mm([dict(out=self.ps_t[:, ub, :], lhsT=Wu[:, c, fi * 128:(fi + 1) * 128], rhs=hn[:, c, :],
                              start=(c == 0), stop=(c == DC - 1)) for c in range(DC)],
                        self.hnb[s] + [Wb], [self.ps[ub]])
                k = fi % 2
                self.op(self.act, lambda: nc.scalar.activation(out=self.silu_t[k][:], in_=self.ps_t[:, gb, :], func=AF.Silu),
                        [self.ps[gb]], [self.silub[k]])
                self.op(self.dve, lambda: nc.vector.tensor_tensor(out=self.act_t[:, fi, :], in0=self.ps_t[:, ub, :],
                                                                  in1=self.silu_t[k][:], op=ALU.mult),
                        [self.ps[ub], self.silub[k]], [self.actb[fi]])
            for dc in range(DC):
                if dc == 4 and first and t + 1 < NT:
                    do_norm(t + 1)
                if dc == 4 and t + 1 == NT and self.hoist and self.hoist[1]:
                    self.hoist[1]()
                db = 4 + dc % 2
                self.mm([dict(out=self.ps_t[:, db, :], lhsT=Wd[:, fi, dc * 128:(dc + 1) * 128], rhs=self.act_t[:, fi, :],
                              start=(fi == 0), stop=(fi == nf - 1)) for fi in range(nf)],
                        self.actb[:nf] + [Wb], [self.ps[db]])
                self.op(self.dve, lambda: nc.vector.scalar_tensor_tensor(out=hT[:, dc, :], in0=self.ps_t[:, db, :], scalar=0.5,
                                                                         in1=hT[:, dc, :], op0=ALU.mult, op1=ALU.add),
                        [self.ps[db], self.hTb[s][dc]], [self.hTb[s][dc]])
            self.store_h(self.hD, self.hDb, t, s)
        if self.hoist:
            self.hoisted = True
            self.hoist = None

    def evac_to_dram(self, bank, scale, dst_ap, dstb):
        nc = self.nc
        k = self.stg_i % 8
        self.stg_i += 1
        self.op(self.act, lambda: nc.scalar.activation(out=self.stg_t[k][:], in_=self.ps_t[:, bank, :], func=AF.Copy, scale=scale),
                [self.ps[bank]], [self.stgb[k]])
        self.dma(self.sp, dst_ap, self.stg_t[k][:], [self.stgb[k]], [dstb], self.stgo[k])

    def proj_fm(self, Wv, col0, s, t, scale, dstD, dstb):
        hn = self.hn_t[s]
        for oc in range(8):
            bank = self.next_bank()
            self.mm([dict(out=self.ps_t[:, bank, :], lhsT=Wv[:, c, col0 + oc * 128:col0 + (oc + 1) * 128], rhs=hn[:, c, :],
                          start=(c == 0), stop=(c == DC - 1)) for c in range(DC)],
                    self.hnb[s] + [self.Wb[self.cur_wslot]], [self.ps[bank]])
            self.evac_to_dram(bank, scale, dstD[oc, :, self.tsl(t)], dstb[t])

    def proj_v(self, Wv, col0, s, t):
        nc = self.nc
        hn = self.hn_t[s]
        for sub in range(4):
            blk = 4 * t + sub
            vs = blk % 2
            for hf in range(2):
                bank = self.next_bank()
                self.mm([dict(out=self.ps_t[:, bank, :], lhsT=hn[:, c, sub * 128:(sub + 1) * 128],
                              rhs=Wv[:, c, col0 + hf * 512:col0 + (hf + 1) * 512], start=(c == 0), stop=(c == DC - 1))
                         for c in range(DC)],
                        self.hnb[s] + [self.Wb[self.cur_wslot]], [self.ps[bank]])
                pv = self.ps_t[:, bank, :].rearrange("p (j e d) -> p j e d", j=4, e=2)
                for e in range(2):
                    self.op(self.dve, lambda: nc.vector.tensor_copy(out=self.vstg_t[vs][:, 4 * hf:4 * hf + 4, 128 * e:128 * e + 64],
                                                                    in_=pv[:, :, e, :]),
                            [self.ps[bank]], [self.vstgb[vs]])
            self.dma(self.sp, self.vD[:, blk].rearrange("h p c -> p h c"), self.vstg_t[vs][:], [self.vstgb[vs]],
                     [self.vDb[t]], self.vstgo[vs])

    def proj_f(self, Wv, s, t):
        nc = self.nc
        hn = self.hn_t[s]
        for sub in range(4):
            blk = 4 * t + sub
            self.mm([dict(out=self.ps_t[:, 6, 0:NH], lhsT=hn[:, c, sub * 128:(sub + 1) * 128], rhs=Wv[:, c, 2048:2048 + NH],
                          start=(c == 0), stop=(c == DC - 1)) for c in range(DC)],
                    self.hnb[s] + [self.Wb[self.cur_wslot]], [self.ps[6]])
            self.op(self.dve, lambda: nc.vector.tensor_tensor(out=self.z_t[:], in0=self.ps_t[:, 6, 0:NH], in1=self.bf_t[:], op=ALU.add),
                    [self.ps[6], self.constb], [self.zb])
            self.op(self.act, lambda: nc.scalar.activation(out=self.z_t[:], in_=self.z_t[:], func=AF.Exp, scale=-1.0),
                    [self.zb], [self.zb])
            self.op(self.act, lambda: nc.scalar.activation(out=self.l_t[:], in_=self.z_t[:], func=AF.Ln, bias=1.0),
                    [self.zb], [self.lb])
            mms = [dict(out=self.ps_t[:, 6, 16:16 + NH], lhsT=self.tri_t[:], rhs=self.l_t[:], start=True, stop=(blk == 0))]
            if blk > 0:
                mms.append(dict(out=self.ps_t[:, 6, 16:16 + NH], lhsT=self.sel_t[:], rhs=self.negc_t[:, blk - 1, :], start=False, stop=True))
            self.mm(mms, [self.lb, self.negcb, self.constb], [self.ps[6]])
            self.op(self.dve, lambda: nc.vector.tensor_copy(out=self.negc_t[:, blk, :], in_=self.ps_t[:, 6, 16:16 + NH]),
                    [self.ps[6]], [self.negcb])
            self.mm([dict(out=self.ps_t[0:NH, 6, 32:160], lhsT=self.negc_t[:, blk, :], rhs=self.identf_t[:], start=True, stop=True)],
                    [self.negcb, self.constb], [self.ps[6]])
            self.op(self.dve, lambda: nc.vector.tensor_scalar(out=self.c1stg_t[:, sub * 128:(sub + 1) * 128], in0=self.ps_t[0:NH, 6, 32:160],
                                                              scalar1=-1.0, scalar2=None, op0=ALU.mult),
                    [self.ps[6]], [self.c1stgb])
        self.dma(self.sp, self.c1D[:, self.tsl(t)], self.c1stg_t[:], [self.c1stgb], [self.c1Db[t]], self.c1stgo)

    def proj_pass(self, wslot, kind, gidx, ncols):
        NT = self.NT
        self.cur_wslot = wslot
        Wv = self.Wview(wslot, 0, 8, ncols)
        if not self.hoisted:
            self.load_h(self.hD, self.hDb, 0, 0)
            self.norm(0, gidx)
        self.hoisted = False
        for t in range(NT):
            s = t % 2
            if t + 1 < NT:
                self.load_h(self.hD, self.hDb, t + 1, (t + 1) % 2)
            elif self.hoist:
                self.hoist[0]()
            if kind == "A":
                self.proj_fm(Wv, 0, s, t, 0.125, self.qD, self.qDb)
                if t + 1 < NT:
                    self.norm((t + 1) % 2, gidx)
                self.proj_fm(Wv, 1024, s, t, 1.0, self.kD, self.kDb)
                self.proj_v(Wv, 2048, s, t)
            elif kind == "B":
                self.proj_fm(Wv, 0, s, t, 0.125, self.qD, self.qDb)
                if t + 1 < NT:
                    self.norm((t + 1) % 2, gidx)
            else:
                self.proj_fm(Wv, 0, s, t, 1.0, self.kD, self.kDb)
                if t + 1 < NT:
                    self.norm((t + 1) % 2, gidx)
                elif self.hoist and self.hoist[1]:
                    self.hoist[1]()
                self.proj_v(Wv, 1024, s, t)
                self.proj_f(Wv, s, t)
        if self.hoist:
            self.hoisted = True
            self.hoist = None

    def out_proj_pass(self, wslot):
        nc, NT = self.nc, self.NT
        Wv = self.Wview(wslot, 0, 8, 1024)
        Wb = self.Wb[wslot]
        self.load_h(self.hD, self.hDb, 0, 0)
        self.load_hn(self.oD, self.oDb, 0, 0)
        for t in range(NT):
            s = t % 2
            if t + 1 < NT:
                self.load_h(self.hD, self.hDb, t + 1, (t + 1) % 2)
                self.load_hn(self.oD, self.oDb, t + 1, (t + 1) % 2)
            if t + 1 == NT and self.hoist:
                self.hoist[0]()
            hT, o = self.hT_t[s], self.hn_t[s]
            for dc in range(DC):
                if dc == 4 and t + 1 == NT and self.hoist and self.hoist[1]:
                    self.hoist[1]()
                bank = self.next_bank()
                self.mm([dict(out=self.ps_t[:, bank, :], lhsT=Wv[:, c, dc * 128:(dc + 1) * 128], rhs=o[:, c, :],
                              start=(c == 0), stop=(c == DC - 1)) for c in range(DC)],
                        self.hnb[s] + [Wb], [self.ps[bank]])
                self.op(self.dve, lambda: nc.vector.tensor_tensor(out=hT[:, dc, :], in0=self.ps_t[:, bank, :], in1=hT[:, dc, :], op=ALU.add),
                        [self.ps[bank], self.hTb[s][dc]], [self.hTb[s][dc]])
            self.store_h(self.hD, self.hDb, t, s)
        if self.hoist:
            self.hoisted = True
            self.hoist = None

    def final_pass(self, do_norm=True):
        NT = self.NT
        if not self.hoisted:
            self.load_h(self.hD, self.hDb, 0, 0)
        self.hoisted = False
        for t in range(NT):
            s = t % 2
            if t + 1 < NT:
                self.load_h(self.hD, self.hDb, t + 1, (t + 1) % 2)
            if do_norm:
                self.norm(s, 13, final=True)
            self.store_h(self.out, self.outb, t, s)

    def attn_pass(self, kind, la):
        nc, NT = self.nc, self.NT
        self.alias_barrier(self.ffn_view, self.att_view)
        pending = []
        state = dict(unit=0, pv=0)

        def load_kv(hp):
            sl = hp % 2
            for kt in range(NT):
                b, o = self.kvb[sl][kt], self.kvo[sl][kt]
                self.dma(self.sp, self.kT_t[sl][:, self.tsl(kt)], self.kD[hp, :, self.tsl(kt)], [self.kDb[kt]], [b], o)
                self.dma(self.sp, self.V_t[sl][:, 4 * kt:4 * kt + 4, :], self.vD[hp, 4 * kt:4 * kt + 4].rearrange("b p c -> p b c"),
                         [self.vDb[kt]], [b], o)
            if kind == "A":
                self.dma(self.pool, self.BT_t[sl][:], self.bt_in[la, hp], [self.inb], [self.BTb[sl]], self.BTo[sl])
                self.op(self.pool, lambda: nc.gpsimd.memset(self.BT_t[sl][0:64, :, 576:640], NEG), [], [self.BTb[sl]])
                self.op(self.pool, lambda: nc.gpsimd.memset(self.BT_t[sl][64:128, :, 0:64], NEG), [], [self.BTb[sl]])

        seq = [(hp, tt) for hp in range(8) for tt in range(NT)]

        def load_q(i):
            hp, tt = seq[i]
            qs = i % 2
            for e in range(2):
                self.dma(self.sp, self.qT_t[qs][e][64 * e:64 * e + 64, :], self.qD[hp, 64 * e:64 * e + 64, self.tsl(tt)],
                         [self.qDb[tt]], [self.qTb[qs]], self.qTo[qs])
            if kind == "B":
                self.dma(self.sp, self.c1_t[qs][0:NH, :], self.c1D[:, self.tsl(tt)], [self.c1Db[tt]], [self.c1b[qs]], self.c1o[qs])

        def flush(keep):
            while len(pending) > keep:
                pending.pop(0)()

        load_kv(0)
        load_q(0)
        for i, (hp, tt) in enumerate(seq):
            sl, qs = hp % 2, i % 2
            if i + 1 < len(seq):
                load_q(i + 1)
            if tt == 1 and hp + 1 < 8:
                load_kv(hp + 1)
            os_ = i % 2
            for e in range(2):
                h = 2 * hp + e
                pb = 64 * e
                if kind == "A":
                    units = []
                    order = [3, 0, 1, 2, 4, 5, 6, 7] if tt > 0 else [4, 5, 6, 7]
                    for j in order:
                        kb = 4 * tt - 4 + j
                        if j <= 3:
                            units.append((kb, 0, 128 * (j + 1)))
                        else:
                            units.append((kb, 128 * (j - 4), 512))
                else:
                    units = [(kb, 0, 512) for kb in range(4 * tt)] + [(4 * tt + j, 128 * j, 512) for j in range(4)]
                pvb = 4 + state["pv"] % 2
                state["pv"] += 1
                nu = len(units)
                for ui, (kb, qlo, qhi) in enumerate(units):
                    n = qhi - qlo
                    u = state["unit"]
                    state["unit"] += 1
                    sbk = u % 4
                    kt = kb // 4
                    mms = [dict(out=self.ps_t[:, sbk, 0:n], lhsT=self.kT_t[sl][:, kb * 128:(kb + 1) * 128],
                                rhs=self.qT_t[qs][e][:, qlo:qhi], start=True, stop=False)]
                    reads = [self.kvb[sl][kt], self.qTb[qs], self.constb]
                    if kind == "A":
                        off = 512 * tt + qlo - 128 * kb
                        mms.append(dict(out=self.ps_t[:, sbk, 0:n], lhsT=self.ident_t[:], rhs=self.BT_t[sl][:, e, off:off + n],
                                        start=False, stop=True))
                        reads.append(self.BTb[sl])
                    else:
                        diag = kb >= 4 * tt
                        mms.append(dict(out=self.ps_t[:, sbk, 0:n], lhsT=self.esel_t[:, h * 128:(h + 1) * 128],
                                        rhs=self.c1_t[qs][:, qlo:qhi], start=False, stop=not diag))
                        if diag:
                            mms.append(dict(out=self.ps_t[:, sbk, 0:128], lhsT=self.ident_t[:], rhs=self.mask_t[:],
                                            start=False, stop=True))
                        reads.append(self.c1b[qs])
                    self.mm(mms, reads, [self.ps[sbk]])
                    if kind == "A":
                        self.op(self.act, lambda: nc.scalar.activation(out=self.pT_t[sbk][:, 0:n], in_=self.ps_t[:, sbk, 0:n], func=AF.Exp),
                                [self.ps[sbk]], [self.pTb[sbk]])
                    else:
                        self.op(self.act, lambda: nc.scalar.activation(out=self.pT_t[sbk][:, 0:n], in_=self.ps_t[:, sbk, 0:n], func=AF.Exp,
                                                                       bias=self.negc_t[:, kb, h:h + 1]),
                                [self.ps[sbk], self.negcb], [self.pTb[sbk]])

                    def pv_fn(sl=sl, kb=kb, kt=kt, e=e, pb=pb, sbk=sbk, n=n, qlo=qlo, qhi=qhi, pvb=pvb, first=(ui == 0),
                              last=(ui == nu - 1), os_=os_, hp=hp, tt=tt):
                        self.mm([dict(out=self.ps_t[:, pvb, qlo:qhi], lhsT=self.V_t[sl][:, kb, 64 * e:64 * e + 128],
                                      rhs=self.pT_t[sbk][:, 0:n], start=first, stop=last)],
                                [self.kvb[sl][kt], self.pTb[sbk]], [self.ps[pvb]])
                        if last:
                            rk = pvb - 4
                            dlo = 64 - pb
                            self.op(self.dve, lambda: nc.vector.reciprocal(out=self.rec_t[rk][pb:pb + 64, :], in_=self.ps_t[dlo:dlo + 64, pvb, :]),
                                    [self.ps[pvb]], [self.recb[rk]])
                            self.op(self.dve, lambda: nc.vector.tensor_tensor(out=self.oT_t[os_][pb:pb + 64, :], in0=self.ps_t[pb:pb + 64, pvb, :],
                                                                              in1=self.rec_t[rk][pb:pb + 64, :], op=ALU.mult),
                                    [self.ps[pvb], self.recb[rk]], [self.oTb[os_]])
                            if e == 1:
                                self.dma(self.sp, self.oD[:, hp, self.tsl(tt)], self.oT_t[os_][:], [self.oTb[os_]], [self.oDb[tt]], self.oTo[os_])

                    pending.append(pv_fn)
                    flush(LOOKAHEAD)
        flush(0)
        self.alias_barrier(self.att_view, self.ffn_view)

    def build(self, debug_stop=None):
        self.setup()
        sets = []
        for L in range(DEPTH):
            for si in range(3):
                sets.append(("ffn", (L, 0, si)))
            if L < N_A:
                sets.append(("projA", L))
                sets.append(("oA", L))
            else:
                sets.append(("projB", L - N_A))
                sets.append(("oB", L - N_A))
            for si in range(3):
                sets.append(("ffn", (L, 1, si)))
            if L == N_A - 1:
                sets.append(("kv", None))

        def wspec(i):
            kind, a = sets[i]
            if kind == "ffn":
                return self.wset_ffn(*a)
            if kind == "projA":
                return self.wset_mat(self.wqkv_in[a], 3072)
            if kind == "oA":
                return self.wset_mat(self.awo_in[a], 1024)
            if kind == "projB":
                return self.wset_mat(self.bwq_in[a], 1024)
            if kind == "oB":
                return self.wset_mat(self.bwo_in[a], 1024)
            return self.wset_mat(self.wkvf_in, 2 * D + NH)

        self.load_wset(wspec(0), 0)
        first_src = True
        for i, (kind, a) in enumerate(sets):
            slot = i % 2
            if i + 1 < len(sets) and not (debug_stop is not None and i >= abs(debug_stop)):
                self.load_wset(wspec(i + 1), (i + 1) % 2)
            self.hoist = None
            stop_here = debug_stop is not None and i == abs(debug_stop)
            if kind in ("ffn", "oA", "oB", "kv") and not stop_here:
                if i + 1 < len(sets):
                    nk, na = sets[i + 1]
                else:
                    nk, na = "final", None
                if nk == "ffn":
                    nL, npos, nsi = na
                    ngi = nL * 2 + npos
                    if nsi == 0:
                        def _prep():
                            self.load_h(self.hD, self.hDb, 0, 0)

                        def _norm(ngi=ngi):
                            self.norm(0, ngi)
                            self.dma(self.sp, self.hnD[:, :, self.tsl(0)], self.hn_t[0][:], self.hnb[0], [self.hnDb[0]], self.hno[0])
                        self.hoist = (_prep, _norm)
                    else:
                        def _prep():
                            self.load_h(self.hD, self.hDb, 0, 0)
                            self.load_hn(self.hnD, self.hnDb, 0, 0)
                        self.hoist = (_prep, None)
                elif nk in ("projA", "projB", "kv"):
                    ngi = {"projA": 8 + (na or 0), "projB": 8 + N_A + (na or 0), "kv": 12}[nk]

                    def _prep():
                        self.load_h(self.hD, self.hDb, 0, 0)

                    def _norm(ngi=ngi):
                        self.norm(0, ngi)
                    self.hoist = (_prep, _norm)
                elif nk == "final" and debug_stop is None:
                    def _prep():
                        self.load_h(self.hD, self.hDb, 0, 0)
                    self.hoist = (_prep, None)
            if kind == "ffn":
                L, pos, si = a
                src, srcb = (self.x_in, self.xb) if first_src else (self.hD, self.hDb)
                first_src = False
                self.ffn_pass(slot, si, L * 2 + pos, src, srcb, first=(si == 0))
            elif kind == "projA":
                self.proj_pass(slot, "A", 8 + a, 3072)
                self.attn_pass("A", a)
            elif kind == "projB":
                self.proj_pass(slot, "B", 8 + N_A + a, 1024)
                self.attn_pass("B", a)
            elif kind in ("oA", "oB"):
                self.out_proj_pass(slot)
            else:
                self.proj_pass(slot, "KV", 12, 2 * D + NH)
            if debug_stop is not None and i == abs(debug_stop):
                break
        if debug_stop is not None and debug_stop < 0:
            nc = self.nc
            for t in range(self.NT):
                s_ = t % 2
                self.load_hn(self.oD, self.oDb, t, s_)
                for c in range(DC):
                    self.op(self.dve, lambda: nc.vector.tensor_copy(out=self.hT_t[s_][:, c, :], in_=self.hn_t[s_][:, c, :]),
                            [self.hnb[s_][c]], [self.hTb[s_][c]])
                self.store_h(self.out, self.outb, t, s_)
        else:
            self.final_pass(do_norm=(debug_stop is None))
        for o in self.hTo:
            self.sp.wait(o.dsem, o.dcnt)
        return self.nc


def _host_inputs(inputs, S):
    f32 = np.float32
    x = np.asarray(inputs["x"], f32)
    B = x.shape[0]
    gains = np.concatenate([
        np.asarray(inputs["ffn_norm"], f32).reshape(8, D),
        np.asarray(inputs["mix_norm"], f32).reshape(4, D),
        np.asarray(inputs["kv_norm"], f32).reshape(1, D),
        np.asarray(inputs["final_norm"], f32).reshape(1, D)], axis=0)
    gains = np.ascontiguousarray(gains.reshape(14, DC, 128).transpose(2, 0, 1).reshape(128, 14 * DC))
    rb = np.asarray(inputs["a_rel_bias"], f32)
    sp = np.arange(128)[:, None]
    tp = np.arange(640)[None, :]
    idx = np.clip(tp - sp, -256, 256) + 256
    bt = rb[:, :, idx]
    bt = np.ascontiguousarray(bt.reshape(N_A, 8, 2, 128, 640).transpose(0, 1, 3, 2, 4))
    bf_bc = np.ascontiguousarray(np.broadcast_to(np.asarray(inputs["b_f_bias"], f32)[None, :], (128, NH)))
    ident = np.eye(128, dtype=f32)
    ones = np.ones((128, 128), f32)
    r = np.arange(128)
    mask = np.where(r[:, None] <= r[None, :], 0.0, NEG).astype(f32)
    tri = (r[:, None] <= r[None, :]).astype(f32)
    sel = np.zeros((128, 128), f32)
    sel[127, :] = 1.0
    esel = np.zeros((NH, NH, 128), f32)
    for h in range(NH):
        esel[h, h, :] = 1.0
    common = dict(
        gains=gains,
        ffn_w_gate=np.asarray(inputs["ffn_w_gate"], f32), ffn_w_up=np.asarray(inputs["ffn_w_up"], f32),
        ffn_w_down=np.asarray(inputs["ffn_w_down"], f32), a_w_qkv=np.asarray(inputs["a_w_qkv"], f32),
        a_w_o=np.asarray(inputs["a_w_o"], f32), bt_full=bt, b_w_kvf=np.asarray(inputs["b_w_kvf"], f32),
        bf_bc=bf_bc, b_w_q=np.asarray(inputs["b_w_q"], f32), b_w_o=np.asarray(inputs["b_w_o"], f32),
        c_ident=ident, c_ones=ones, c_mask=mask, c_tri=tri, c_sel=sel, c_esel=esel.reshape(NH, NH * 128))
    maps = []
    for b in range(B):
        xT = np.ascontiguousarray(x[b].reshape(S, DC, 128).transpose(2, 1, 0))
        m = dict(common)
        m["xT"] = xT
        maps.append(m)
    return maps


_CACHE = {}


def run(inputs, S, n_cores, debug_stop=None):
    key = (S, debug_stop)
    if key not in _CACHE:
        _CACHE[key] = Prog(S).build(debug_stop)
    nc = _CACHE[key]
    maps = _host_inputs(inputs, S)[:n_cores]
    res = run_bass_kernel_spmd(nc, maps, core_ids=list(range(n_cores)))
    outs = []
    for r in res.results:
        oT = np.asarray(r["outT"])
        outs.append(oT.transpose(2, 1, 0).reshape(S, D))
    return np.stack(outs, axis=0).astype(np.float32)


def kernel(**inputs):
    return run(inputs, SEQ, 8)
```
